# Optimizing a Trainium2 kernel written in Bass

```python
import math
import functools
import jax
import jax.numpy as jnp
from jax import lax
import numpy as np

D_MODEL = 2048
BATCH = 8
SEQ = 2048
DEPTH = 4

CHUNK = 64
N_MIXERS = 3
EPS = 1e-6

HG_F_DIM = 128
HG_HEADS = D_MODEL // HG_F_DIM
HG_I_DIM = D_MODEL // HG_HEADS
HG_FORGET = HG_HEADS * HG_F_DIM
HG_WIDTH = HG_HEADS * HG_I_DIM
HG_IN = 2 * HG_FORGET + 2 * HG_WIDTH

MLA_NOPE = 128
MLA_ROPE = 64
MLA_V = 128
MLA_QK = MLA_NOPE + MLA_ROPE
MLA_HEADS = D_MODEL // MLA_V
MLA_Q_RANK = D_MODEL // 4
MLA_KV_RANK = D_MODEL // 4
MLA_WIDTH = MLA_HEADS * MLA_V
MLA_IN = MLA_Q_RANK + MLA_KV_RANK + MLA_ROPE + MLA_WIDTH
ROPE_THETA = 10000.0
Q_BLOCK = 128

GDN_DK = 128
GDN_DV = 128
GDN_QK_HEADS = D_MODEL // GDN_DK
GDN_V_HEADS = 2 * GDN_QK_HEADS
GDN_KEY_WIDTH = GDN_QK_HEADS * GDN_DK
GDN_WIDTH = GDN_V_HEADS * GDN_DV
GDN_CONV_CH = 2 * GDN_KEY_WIDTH + GDN_WIDTH
GDN_IN = GDN_CONV_CH + GDN_WIDTH + 2 * GDN_V_HEADS
CONV_WIDTH = 4

kernel_name = 'hybrid_hgrn2_mla_gdn_sandwich_trunk'


def rms_norm(x, w):
    xf = x.astype(jnp.float32)
    y = xf * lax.rsqrt(jnp.mean(xf * xf, axis=-1, keepdims=True) + EPS)
    return (y * w.astype(jnp.float32)).astype(x.dtype)


def to_chunks(t):
    b, s, h, d = t.shape
    return t.reshape(b, s // CHUNK, CHUNK, h, d).transpose(1, 0, 3, 2, 4)


def from_chunks(t):
    nc, b, h, c, d = t.shape
    return t.transpose(1, 0, 3, 2, 4).reshape(b, nc * c, h, d)


def hgrn_lower_bounds(lb_param):
    p = jax.nn.softmax(lb_param.astype(jnp.float32), axis=0)
    c = lax.cumsum(p, axis=0)
    return c - c[0:1]


def gla_chunked(q, k, v, log_f):
    f32 = jnp.float32
    qc, kc, vc, gc = (to_chunks(t.astype(f32)) for t in (q, k, v, log_f))
    bcum = lax.cumsum(gc, axis=3)
    tri = jnp.tril(jnp.ones((CHUNK, CHUNK), dtype=bool))

    def step(state, inp):
        qi, ki, vi, bi = inp
        rel = jnp.where(tri[:, :, None], bi[:, :, :, None, :] - bi[:, :, None, :, :], -jnp.inf)
        scores = jnp.einsum('bhtk,bhsk,bhtsk->bhts', qi, ki, jnp.exp(rel))
        b_last = bi[:, :, -1, :]
        out = (jnp.einsum('bhts,bhsv->bhtv', scores, vi)
               + jnp.einsum('bhtk,bhkv->bhtv', qi * jnp.exp(bi), state))
        new_state = (state * jnp.exp(b_last)[..., None]
                     + jnp.einsum('bhsk,bhsv->bhkv', ki * jnp.exp(b_last[:, :, None, :] - bi), vi))
        return new_state, out

    b, h = q.shape[0], q.shape[2]
    state0 = jnp.zeros((b, h, q.shape[-1], v.shape[-1]), f32)
    _, out = lax.scan(step, state0, (qc, kc, vc, bcum))
    return from_chunks(out)


def gated_delta_chunked(q, k, v, g, beta):
    f32 = jnp.float32
    qc, kc, vc = (to_chunks(t.astype(f32)) for t in (q, k, v))
    gc = to_chunks(g.astype(f32)[..., None])[..., 0]
    bc = to_chunks(beta.astype(f32)[..., None])[..., 0]
    gcum = lax.cumsum(gc, axis=3)
    tri = jnp.tril(jnp.ones((CHUNK, CHUNK), dtype=bool))
    strict = jnp.tril(jnp.ones((CHUNK, CHUNK), dtype=bool), k=-1)
    decay = jnp.exp(jnp.where(tri, gcum[..., :, None] - gcum[..., None, :], -jnp.inf))
    k_beta = kc * bc[..., None]
    lower = jnp.where(strict, jnp.einsum('nbhtk,nbhsk->nbhts', k_beta, kc) * decay, 0.0)
    a_mat = lower + jnp.eye(CHUNK, dtype=f32)
    solve = functools.partial(lax.linalg.triangular_solve, left_side=True, lower=True,
                              unit_diagonal=True)
    u = solve(a_mat, vc * bc[..., None])
    w = solve(a_mat, k_beta * jnp.exp(gcum)[..., None])

    def step(state, inp):
        qi, ki, ui, wi, gi, di = inp
        v_new = ui - jnp.einsum('bhtk,bhkv->bhtv', wi, state)
        attn = jnp.einsum('bhtk,bhsk->bhts', qi, ki) * di
        out = (jnp.einsum('bhtk,bhkv->bhtv', qi * jnp.exp(gi)[..., None], state)
               + jnp.einsum('bhts,bhsv->bhtv', attn, v_new))
        g_last = gi[:, :, -1]
        new_state = (state * jnp.exp(g_last)[..., None, None]
                     + jnp.einsum('bhsk,bhsv->bhkv',
                                  ki * jnp.exp(g_last[..., None] - gi)[..., None], v_new))
        return new_state, out

    b, h = q.shape[0], q.shape[2]
    state0 = jnp.zeros((b, h, q.shape[-1], v.shape[-1]), f32)
    _, out = lax.scan(step, state0, (qc, kc, u, w, gcum, decay))
    return from_chunks(out)


def apply_rope(t, cos, sin):
    half = t.shape[-1] // 2
    t1, t2 = t[..., :half], t[..., half:]
    return jnp.concatenate([t1 * cos - t2 * sin, t2 * cos + t1 * sin], axis=-1)


def causal_conv(x, w):
    s = x.shape[1]
    xp = jnp.pad(x, ((0, 0), (CONV_WIDTH - 1, 0), (0, 0)))
    y = xp[:, 0:s] * w[0]
    for j in range(1, CONV_WIDTH):
        y = y + xp[:, j:j + s] * w[j]
    return y


def hgrn2_mixer(u, w_in, head_norm, w_out, lower_bound):
    b, s, _ = u.shape
    proj = u @ w_in
    q, f, i, z = jnp.split(proj, [HG_FORGET, 2 * HG_FORGET, 2 * HG_FORGET + HG_WIDTH], axis=-1)
    f = f.astype(jnp.float32)
    lb = lower_bound.astype(jnp.float32)
    log_forget = jnp.logaddexp(jnp.log(lb), jnp.log1p(-lb) + jax.nn.log_sigmoid(f))
    k = (1.0 - lb) * jax.nn.sigmoid(-f)
    q = jax.nn.silu(q).reshape(b, s, HG_HEADS, HG_F_DIM) * HG_F_DIM ** -0.5
    o = gla_chunked(q, k.reshape(b, s, HG_HEADS, HG_F_DIM),
                    i.reshape(b, s, HG_HEADS, HG_I_DIM),
                    log_forget.reshape(b, s, HG_HEADS, HG_F_DIM))
    o = rms_norm(o, head_norm).astype(u.dtype).reshape(b, s, HG_WIDTH)
    return (o * jax.nn.silu(z)) @ w_out


def mla_mixer(u, positions, w_in, q_norm, kv_norm, w_uq, w_ukv, w_out):
    b, s, _ = u.shape
    proj = u @ w_in
    c_q, c_kv, k_rope, z = jnp.split(
        proj, [MLA_Q_RANK, MLA_Q_RANK + MLA_KV_RANK, MLA_Q_RANK + MLA_KV_RANK + MLA_ROPE], axis=-1)
    q = (rms_norm(c_q, q_norm) @ w_uq).reshape(b, s, MLA_HEADS, MLA_QK)
    kv = (rms_norm(c_kv, kv_norm) @ w_ukv).reshape(b, s, MLA_HEADS, MLA_NOPE + MLA_V)
    q_nope, q_rope = q[..., :MLA_NOPE], q[..., MLA_NOPE:]
    k_nope, v = kv[..., :MLA_NOPE], kv[..., MLA_NOPE:]
    inv_freq = ROPE_THETA ** (-jnp.arange(0, MLA_ROPE, 2, dtype=jnp.float32) / MLA_ROPE)
    ang = positions.astype(jnp.float32)[..., None] * inv_freq
    cos, sin = jnp.cos(ang), jnp.sin(ang)
    q_rope = apply_rope(q_rope, cos[:, :, None, :], sin[:, :, None, :])
    k_rope = apply_rope(k_rope, cos, sin)
    scale = MLA_QK ** -0.5
    n_blocks = s // Q_BLOCK
    key_chunk = jnp.arange(s) // CHUNK

    def blocks(t):
        return t.reshape(b, n_blocks, Q_BLOCK, MLA_HEADS, t.shape[-1]).transpose(1, 0, 2, 3, 4)

    def attend(args):
        blk, qn, qr = args
        scores = (jnp.einsum('bqhd,bkhd->bhqk', qn, k_nope)
                  + jnp.einsum('bqhr,bkr->bhqk', qr, k_rope)) * scale
        q_chunk = (blk * Q_BLOCK + jnp.arange(Q_BLOCK)) // CHUNK
        mask = key_chunk[None, :] <= q_chunk[:, None]
        scores = jnp.where(mask, scores.astype(jnp.float32), -jnp.inf)
        p = jax.nn.softmax(scores, axis=-1).astype(v.dtype)
        return jnp.einsum('bhqk,bkhd->bqhd', p, v)

    out = lax.map(attend, (jnp.arange(n_blocks), blocks(q_nope), blocks(q_rope)))
    o = out.transpose(1, 0, 2, 3, 4).reshape(b, s, MLA_WIDTH)
    return (o * jax.nn.silu(z)) @ w_out


def gdn_mixer(u, w_in, conv_w, a_log, dt_bias, head_norm, w_out):
    b, s, _ = u.shape
    proj = u @ w_in
    qkv, z, a, beta_logit = jnp.split(
        proj, [GDN_CONV_CH, GDN_CONV_CH + GDN_WIDTH, GDN_CONV_CH + GDN_WIDTH + GDN_V_HEADS], axis=-1)
    qkv = jax.nn.silu(causal_conv(qkv, conv_w))
    q, k, v = jnp.split(qkv, [GDN_KEY_WIDTH, 2 * GDN_KEY_WIDTH], axis=-1)
    rep = GDN_V_HEADS // GDN_QK_HEADS

    def l2n(t):
        tf = t.astype(jnp.float32).reshape(b, s, GDN_QK_HEADS, GDN_DK)
        tf = tf * lax.rsqrt(jnp.sum(tf * tf, axis=-1, keepdims=True) + EPS)
        return jnp.repeat(tf, rep, axis=2)

    q = l2n(q) * GDN_DK ** -0.5
    k = l2n(k)
    v = v.reshape(b, s, GDN_V_HEADS, GDN_DV)
    beta = jax.nn.sigmoid(beta_logit.astype(jnp.float32))
    g = -jnp.exp(a_log.astype(jnp.float32)) * jax.nn.softplus(
        a.astype(jnp.float32) + dt_bias.astype(jnp.float32))
    o = gated_delta_chunked(q, k, v, g, beta)
    o = rms_norm(o, head_norm).astype(u.dtype)
    o = o * jax.nn.silu(z.reshape(b, s, GDN_V_HEADS, GDN_DV))
    return o.reshape(b, s, GDN_WIDTH) @ w_out


def setup_inputs(seed: int = 0) -> dict:
    key = jax.random.key(seed)
    keys = iter(jax.random.split(key, 64))

    def nrm(shape, scale):
        return jax.random.normal(next(keys), shape, jnp.float32) * scale

    def gain(n):
        return 1.0 + nrm((n,), 0.02)

    inp = {}
    inp['x'] = nrm((BATCH, SEQ, D_MODEL), 1.0)
    offset = jax.random.randint(next(keys), (BATCH, 1), 0, 4096, dtype=jnp.int32)
    inp['positions'] = offset + jnp.arange(SEQ, dtype=jnp.int32)[None, :]
    inp['hgrn_lb'] = nrm((DEPTH, HG_FORGET), 0.1)
    for i in range(DEPTH):
        kind = i % N_MIXERS
        p = 'l%d_' % i
        inp[p + 'pre_norm'] = gain(D_MODEL)
        inp[p + 'post_norm'] = gain(D_MODEL)
        if kind == 0:
            inp[p + 'w_in'] = nrm((D_MODEL, HG_IN), D_MODEL ** -0.5)
            inp[p + 'head_norm'] = gain(HG_I_DIM)
            inp[p + 'w_out'] = nrm((HG_WIDTH, D_MODEL), HG_WIDTH ** -0.5)
        elif kind == 1:
            inp[p + 'w_in'] = nrm((D_MODEL, MLA_IN), D_MODEL ** -0.5)
            inp[p + 'q_norm'] = gain(MLA_Q_RANK)
            inp[p + 'kv_norm'] = gain(MLA_KV_RANK)
            inp[p + 'w_uq'] = nrm((MLA_Q_RANK, MLA_HEADS * MLA_QK), MLA_Q_RANK ** -0.5)
            inp[p + 'w_ukv'] = nrm((MLA_KV_RANK, MLA_HEADS * (MLA_NOPE + MLA_V)), MLA_KV_RANK ** -0.5)
            inp[p + 'w_out'] = nrm((MLA_WIDTH, D_MODEL), MLA_WIDTH ** -0.5)
        else:
            inp[p + 'w_in'] = nrm((D_MODEL, GDN_IN), D_MODEL ** -0.5)
            inp[p + 'conv_w'] = nrm((CONV_WIDTH, GDN_CONV_CH), CONV_WIDTH ** -0.5)
            inp[p + 'a_log'] = jnp.log(jax.random.uniform(next(keys), (GDN_V_HEADS,), jnp.float32, 1.0, 16.0))
            dt = jnp.exp(jax.random.uniform(next(keys), (GDN_V_HEADS,), jnp.float32,
                                            math.log(1e-3), math.log(1e-1)))
            inp[p + 'dt_bias'] = dt + jnp.log(-jnp.expm1(-dt))
            inp[p + 'head_norm'] = gain(GDN_DV)
            inp[p + 'w_out'] = nrm((GDN_WIDTH, D_MODEL), GDN_WIDTH ** -0.5)
    return inp


def reference(x, positions, hgrn_lb,
              l0_pre_norm, l0_post_norm, l0_w_in, l0_head_norm, l0_w_out,
              l1_pre_norm, l1_post_norm, l1_w_in, l1_q_norm, l1_kv_norm, l1_w_uq, l1_w_ukv, l1_w_out,
              l2_pre_norm, l2_post_norm, l2_w_in, l2_conv_w, l2_a_log, l2_dt_bias, l2_head_norm, l2_w_out,
              l3_pre_norm, l3_post_norm, l3_w_in, l3_head_norm, l3_w_out):
    lower_bounds = hgrn_lower_bounds(hgrn_lb)
    layers = [
        (l0_pre_norm, l0_post_norm, (l0_w_in, l0_head_norm, l0_w_out)),
        (l1_pre_norm, l1_post_norm, (l1_w_in, l1_q_norm, l1_kv_norm, l1_w_uq, l1_w_ukv, l1_w_out)),
        (l2_pre_norm, l2_post_norm, (l2_w_in, l2_conv_w, l2_a_log, l2_dt_bias, l2_head_norm, l2_w_out)),
        (l3_pre_norm, l3_post_norm, (l3_w_in, l3_head_norm, l3_w_out)),
    ]
    h = x
    for i in range(DEPTH):
        pre, post, params = layers[i]
        u = rms_norm(h, pre)
        kind = i % N_MIXERS
        if kind == 0:
            y = hgrn2_mixer(u, *params, lower_bound=lower_bounds[i])
        elif kind == 1:
            y = mla_mixer(u, positions, *params)
        else:
            y = gdn_mixer(u, *params)
        h = h + rms_norm(y, post)
    return h
```

```python
import contextlib
import numpy as np
import concourse.bass as bass
import concourse.mybir as mybir
from concourse.bass_utils import run_bass_kernel_spmd

F32 = mybir.dt.float32
F32R = mybir.dt.float32r
BF16 = mybir.dt.bfloat16
I32 = mybir.dt.int32
AF = mybir.ActivationFunctionType
ALU = mybir.AluOpType

S = 2048
D = 2048
EPS = 1e-6
NDMA = 8
SAME_ENGINE_SYNC = True


class Reg:
    __slots__ = ("w", "r")

    def __init__(self):
        self.w = None
        self.r = {}


class V:
    __slots__ = ("ap", "buf", "key")

    def __init__(self, ap, buf, key):
        self.ap = ap
        self.buf = buf
        self.key = key


class Buf:
    def __init__(self, t, excl=False):
        self.t = t
        self.regs = {}
        self.whole = Reg()
        self.excl = excl

    def __getitem__(self, idx):
        return V(self.t[idx], self, None)

    def k(self, key, idx):
        return V(self.t[idx], self, None if self.excl else key)

    def reg(self, key):
        if key not in self.regs:
            self.regs[key] = Reg()
        return self.regs[key]


def vv(v, ap):
    return V(ap, v.buf, v.key)


class KB:
    def __init__(self, nc, es):
        self.nc = nc
        self.es = es
        self.eng = {"pe": nc.tensor, "act": nc.scalar, "dve": nc.vector, "pool": nc.gpsimd, "sp": nc.sync}
        self.sem = {}
        self.cnt = {}
        self.epoch = {}
        self.seen = {e: {} for e in self.eng}
        self.top_es = es
        for e in self.eng:
            self.epoch[e] = 0
            self.sem[("E", e, 0)] = es.enter_context(nc.semaphore("s_%s_0" % e))
            self.cnt[e] = 0
        self.dcnt = {"sp": 0, "pool": 0}
        for q in ("sp", "pool"):
            for k in range(NDMA):
                self.sem[("D", q, k)] = es.enter_context(nc.semaphore("d_%s%d" % (q, k)))
        self.nbuf = 0

    def sb(self, shape, dt, name=None):
        self.nbuf += 1
        return Buf(self.es.enter_context(self.nc.sbuf_tensor(name or "b%d" % self.nbuf, shape, dt)))

    def ps(self, shape, dt, name=None):
        self.nbuf += 1
        return Buf(self.es.enter_context(self.nc.psum_tensor(name or "p%d" % self.nbuf, shape, dt)), excl=True)

    def dram(self, t):
        return Buf(t)

    def _deps(self, reads, writes):
        deps = {}

        def add(tok):
            if tok is None:
                return
            k, n = tok
            if deps.get(k, 0) < n:
                deps[k] = n

        for v in reads:
            b = v.buf
            add(b.whole.w)
            if v.key is None:
                for r in b.regs.values():
                    add(r.w)
            else:
                add(b.reg(v.key).w)
            if b.excl:
                for k, n in b.whole.r.items():
                    add((k, n))
        for v in writes:
            b = v.buf
            regs = [b.whole] + (list(b.regs.values()) if v.key is None else [b.reg(v.key)])
            for r in regs:
                add(r.w)
                for k, n in r.r.items():
                    add((k, n))
        return deps

    def _wait(self, e, deps):
        for k, n in deps.items():
            if k[0] == "E" and k[1] == e and not (SAME_ENGINE_SYNC and e in ("act", "dve", "pool")):
                continue
            if self.seen[e].get(k, 0) >= n:
                continue
            self.eng[e].wait_ge(self.sem[k], n)
            self.seen[e][k] = n

    def _record(self, tok, reads, writes):
        k, n = tok
        for v in reads:
            r = v.buf.whole if v.key is None else v.buf.reg(v.key)
            if r.r.get(k, 0) < n:
                r.r[k] = n
        for v in writes:
            r = v.buf.whole if v.key is None else v.buf.reg(v.key)
            r.w = tok
            r.r = {}

    def op(self, e, fn, reads, writes):
        reads = [v for v in reads if isinstance(v, V)]
        self._wait(e, self._deps(reads, writes))
        inst = fn()
        if self.cnt[e] >= 30000:
            self.epoch[e] += 1
            self.cnt[e] = 0
            self.sem[("E", e, self.epoch[e])] = self.top_es.enter_context(
                self.nc.semaphore("s_%s_%d" % (e, self.epoch[e])))
        self.cnt[e] += 1
        key = ("E", e, self.epoch[e])
        inst.then_inc(self.sem[key], 1)
        self._record((key, self.cnt[e]), reads, writes)

    def barrier(self):
        for e in self.eng:
            deps = {}
            for o in self.eng:
                if o != e and self.cnt[o] > 0:
                    deps[("E", o, self.epoch[o])] = self.cnt[o]
            self._wait(e, deps)
            self.wait_all(e)

    def dma(self, q, out, in_, **kw):
        deps = self._deps([in_], [out])
        i = self.dcnt[q]
        self.dcnt[q] += 1
        k = i % NDMA
        key = ("D", q, k)
        if i >= NDMA:
            deps[key] = max(deps.get(key, 0), 16 * (i // NDMA))
        self._wait(q, deps)
        inst = self.eng[q].dma_start(out=out.ap, in_=in_.ap, **kw)
        inst.then_inc(self.sem[key], 16)
        self._record((key, 16 * (i // NDMA + 1)), [in_], [out])

    def wait_all(self, e):
        deps = {}
        for q in ("sp", "pool"):
            i = self.dcnt[q]
            for k in range(NDMA):
                n = (i - k + NDMA - 1) // NDMA if i > k else 0
                if n > 0:
                    deps[("D", q, k)] = 16 * n
        self._wait(e, deps)

    def mm(self, out, lhsT, rhs, start=True, stop=True):
        self.op("pe", lambda: self.nc.tensor.matmul(out.ap, lhsT=lhsT.ap, rhs=rhs.ap, start=start, stop=stop),
                [lhsT, rhs], [out])

    def tr(self, out, in_, ident):
        self.op("pe", lambda: self.nc.tensor.transpose(out.ap, in_.ap, ident.ap), [in_, ident], [out])

    def act(self, out, in_, func, bias=None, scale=None, accum=None):
        kw = {}
        rd = [in_]
        if bias is not None:
            kw["bias"] = bias.ap if isinstance(bias, V) else bias
            rd.append(bias)
        if scale is not None:
            kw["scale"] = scale.ap if isinstance(scale, V) else scale
            rd.append(scale)
        wr = [out]
        if accum is not None:
            kw["accum_out"] = accum.ap
            wr.append(accum)
        self.op("act", lambda: self.nc.scalar.activation(out=out.ap, in_=in_.ap, func=func, **kw), rd, wr)

    def _e(self, e):
        return self.nc.vector if e == "dve" else self.nc.gpsimd

    def tt(self, out, a, b, op, e="dve"):
        self.op(e, lambda: self._e(e).tensor_tensor(out=out.ap, in0=a.ap, in1=b.ap, op=op), [a, b], [out])

    def ts(self, out, a, s1, s2, op0, op1=None, e="dve"):
        g = lambda s: s.ap if isinstance(s, V) else s
        if op1 is None:
            f = lambda: self._e(e).tensor_scalar(out=out.ap, in0=a.ap, scalar1=g(s1), scalar2=None, op0=op0)
        else:
            f = lambda: self._e(e).tensor_scalar(out=out.ap, in0=a.ap, scalar1=g(s1), scalar2=g(s2), op0=op0, op1=op1)
        self.op(e, f, [a, s1, s2], [out])

    def stt(self, out, a, s, b, op0, op1):
        g = s.ap if isinstance(s, V) else s
        self.op("dve", lambda: self.nc.vector.scalar_tensor_tensor(out=out.ap, in0=a.ap, scalar=g, in1=b.ap, op0=op0, op1=op1),
                [a, s, b], [out])

    def copy(self, out, a, e="dve"):
        if e == "act":
            self.op("act", lambda: self.nc.scalar.copy(out=out.ap, in_=a.ap), [a], [out])
        else:
            self.op(e, lambda: self._e(e).tensor_copy(out=out.ap, in_=a.ap), [a], [out])

    def memset(self, out, val, e="dve"):
        self.op(e, lambda: self._e(e).memset(out.ap, val), [], [out])

    def recip(self, out, a):
        self.op("dve", lambda: self.nc.vector.reciprocal(out=out.ap, in_=a.ap), [a], [out])

    def recip_fast(self, out, a):
        self.op("dve", lambda: self.nc.vector.reciprocal_approx_fast(out=out.ap, in_=a.ap), [a], [out])

    def scan(self, out, d0, d1, init=0.0):
        self.op("dve", lambda: self.nc.vector.tensor_tensor_scan(out=out.ap, data0=d0.ap, data1=d1.ap, initial=init,
                                                                  op0=ALU.mult, op1=ALU.add), [d0, d1], [out])


NCONST = 705 + 2048


def make_consts():
    c = np.zeros((128, NCONST), np.float32)
    i = np.arange(128)
    c[:, 0:128] = np.eye(128)
    same = (i[:, None] // 64) == (i[None, :] // 64)
    c[:, 128:256] = same & (i[:, None] <= i[None, :])
    c[:, 256:384] = same & (i[:, None] < i[None, :])
    c[:, 384:512] = 1.0
    c[:, 512:640] = same & (i[:, None] > i[None, :])
    R = np.zeros((64, 64), np.float32)
    for a in range(32):
        R[a, 32 + a] = -1.0
        R[32 + a, a] = 1.0
    c[0:64, 640:704] = R.T
    inv_freq = 10000.0 ** (-np.arange(0, 64, 2, dtype=np.float32) / 64)
    c[0:64, 704] = np.concatenate([inv_freq, inv_freq])
    m = np.ones(2048, np.float32)
    m[0::64] = 0.0
    c[:, 705:705 + 2048] = m[None, :]
    sel = np.zeros((32, 32 * 128), np.float32)
    for h in range(32):
        sel[h, h * 128:(h + 1) * 128] = 1.0
    return c, sel


LAYER_KIND = [0, 1, 2, 0]


class Prog:
    def __init__(self, layers):
        self.layers = layers
        nc = bass.Bass("TRN2", target_bir_lowering=False)
        self.nc = nc
        self.es = contextlib.ExitStack()
        self.build()

    def din(self, name, shape, dt=F32):
        return self.kb.dram(self.nc.dram_tensor(name, list(shape), dt, kind="ExternalInput").ap())

    def build(self):
        nc = self.nc
        with self.es as es:
            kb = self.kb = KB(nc, es)
            self.x = self.din("x", [S, D])
            self.pos = self.din("pos", [1, S], I32)
            self.cst_d = self.din("consts", [128, NCONST])
            self.sel_d = self.din("sel", [32, 4096])
            self.lb_d = self.din("lbp", [128, 64])
            self.W = {}
            for l in range(4):
                p = "l%d_" % l
                self.W[p + "pre"] = self.din(p + "pre", [128, 16])
                self.W[p + "post"] = self.din(p + "post", [1, D])
                kind = LAYER_KIND[l]
                if kind == 0:
                    self.W[p + "w_in"] = self.din(p + "w_in", [D, 8192])
                    self.W[p + "hn"] = self.din(p + "hn", [128, 1])
                    self.W[p + "w_out"] = self.din(p + "w_out", [2048, D])
                elif kind == 1:
                    self.W[p + "w_in"] = self.din(p + "w_in", [D, 3136])
                    self.W[p + "qn"] = self.din(p + "qn", [128, 4])
                    self.W[p + "kvn"] = self.din(p + "kvn", [128, 4])
                    self.W[p + "w_uq"] = self.din(p + "w_uq", [512, 3072])
                    self.W[p + "w_ukv"] = self.din(p + "w_ukv", [512, 4096])
                    self.W[p + "w_out"] = self.din(p + "w_out", [2048, D])
                else:
                    self.W[p + "w_in"] = self.din(p + "w_in", [D, 12352])
                    self.W[p + "conv"] = self.din(p + "conv", [128, 256])
                    self.W[p + "alog"] = self.din(p + "alog", [32, 1])
                    self.W[p + "dtb"] = self.din(p + "dtb", [32, 1])
                    self.W[p + "hn"] = self.din(p + "hn", [128, 1])
                    self.W[p + "w_out"] = self.din(p + "w_out", [4096, D])
            self.out = kb.dram(nc.dram_tensor("out", [S, D], F32, kind="ExternalOutput").ap())
            self.og = kb.dram(nc.dram_tensor("og_scr", [4096, S], BF16).ap())

            self.cst = kb.sb([128, NCONST], F32, "cst")
            self.cstb = kb.sb([128, 640], BF16, "cstb")
            self.small = kb.sb([128, 64], F32, "small")
            self.epsb = kb.sb([128, 1], F32, "epsb")
            self.pb = [kb.ps([128, 512], F32, "psum%d" % i) for i in range(8)]
            kb.dma("sp", self.cst[:, :], self.cst_d[:, :])
            kb.copy(self.cstb[:, :], self.cst[:, 0:640])
            kb.memset(self.epsb[:, :], EPS)
            self.onesF1 = kb.sb([128, 1], F32, "onesF1")
            kb.memset(self.onesF1[:, :], 1.0)
            self.identF = self.cst[:, 0:128]
            self.identB = self.cstb[:, 0:128]
            self.maskHG = self.cst[:, 128:256]
            self.maskSU = self.cst[:, 256:384]
            self.onesB = self.cstb[:, 384:512]
            self.scanmask = self.cst[:, 705:705 + 2048]

            first = True
            for l in self.layers:
                src = self.x if first else self.out
                first = False
                with contextlib.ExitStack() as lab:
                    kb.es = lab
                    self.uT = kb.sb([128, 16, S], BF16)
                    with contextlib.ExitStack() as les:
                        kb.es = les
                        self.phase_a(l, src)
                        kb.barrier()
                    with contextlib.ExitStack() as les:
                        kb.es = les
                        kind = LAYER_KIND[l]
                        if kind == 0:
                            self.hgrn(l)
                            nchunk = 16
                        elif kind == 1:
                            self.mla(l)
                            nchunk = 16
                        else:
                            self.gdn(l)
                            nchunk = 32
                        kb.barrier()
                with contextlib.ExitStack() as les:
                    kb.es = les
                    self.phase_c(l, src, nchunk)
                    kb.barrier()
                kb.es = es
            kb.wait_all("sp")
            kb.wait_all("pool")

    def rstd_from_ss(self, rs, ss, n):
        kb = self.kb
        P = rs.ap.shape[0]
        kb.act(rs, ss, AF.Sqrt, bias=vv(self.epsb[:, :], self.epsb.t[0:P, :]), scale=1.0 / n)
        kb.recip(rs, rs)

    def phase_a(self, l, src):
        kb = self.kb
        p = "l%d_" % l
        pre = kb.sb([128, 16], F32)
        kb.dma("sp", pre[:, :], self.W[p + "pre"][:, :])
        hb = [kb.sb([128, D], F32) for _ in range(2)]
        junk = kb.sb([128, D], BF16)
        ss = kb.sb([128, 2], F32)
        for i in range(16):
            ht = hb[i % 2]
            kb.dma("sp", ht[:, :], src.k(i, (slice(i * 128, (i + 1) * 128), slice(None))))
            kb.act(junk[:, :], ht[:, :], AF.Square, accum=ss[:, 0:1])
            self.rstd_from_ss(ss[:, 1:2], ss[:, 0:1], D)
            kb.ts(ht[:, :], ht[:, :], ss[:, 1:2], None, ALU.mult)
            for c4 in range(4):
                pbk = self.pb[c4 % 2]
                for j in range(4):
                    c = c4 * 4 + j
                    kb.tr(pbk[:, j * 128:(j + 1) * 128], ht[:, c * 128:(c + 1) * 128], self.identF)
                o = self.uT.k(("a", i), (slice(None), slice(c4 * 4, c4 * 4 + 4), slice(i * 128, (i + 1) * 128)))
                a = vv(pbk[:, :], pbk.t[:, :].rearrange("p (a b) -> p a b", b=128))
                b = vv(pre[:, :], pre.t[:, c4 * 4:c4 * 4 + 4].unsqueeze(2).to_broadcast([128, 4, 128]))
                kb.tt(o, a, b, ALU.mult)

    def phase_c(self, l, src, nchunk):
        kb = self.kb
        p = "l%d_" % l
        wo = self.W[p + "w_out"]
        C = nchunk
        postb = kb.sb([128, D], F32)
        kb.dma("sp", postb[:, :], vv(self.W[p + "post"][:, :], self.W[p + "post"].t[0:1, :].partition_broadcast(128).squeeze(1)))
        nog = 2 if C == 16 else 1
        ogs = [kb.sb([128, C, 512], BF16) for _ in range(nog)]
        if C == 16:
            wres = kb.sb([128, 16, D], BF16)
            for cb in range(4):
                kb.dma("pool", wres.k(cb, (slice(None), slice(None), slice(cb * 512, (cb + 1) * 512))),
                       vv(wo[:, :], wo.t[:, cb * 512:(cb + 1) * 512].rearrange("(c p) n -> p c n", p=128)))
        else:
            wbs = [kb.sb([128, 16, 512], BF16) for _ in range(4)]
        yb = kb.sb([128, 4, D], F32)
        hb = [kb.sb([128, D], F32) for _ in range(2)]
        junk = kb.sb([128, D], BF16)
        ss = kb.sb([128, 2], F32)
        wi = 0
        nh = C // 16
        for g in range(4):
            ogt = ogs[g % nog]
            kb.dma("sp", ogt[:, :, :], vv(self.og[:, :], self.og.t[0:C * 128, g * 512:(g + 1) * 512].rearrange("(c p) t -> p c t", p=128)))
            for cb in range(4):
                slabs = []
                if C != 16:
                    for hh in range(nh):
                        wb = wbs[wi % 4]
                        wi += 1
                        kb.dma("pool", wb[:, :, :], vv(wo[:, :], wo.t[hh * 2048:(hh + 1) * 2048, cb * 512:(cb + 1) * 512].rearrange("(c p) n -> p c n", p=128)))
                        slabs.append(wb)
                for t in range(4):
                    pbk = self.pb[(cb * 4 + t) % 4]
                    for c in range(C):
                        if C == 16:
                            wv_ = wres.k(cb, (slice(None), c, slice(cb * 512, (cb + 1) * 512)))
                        else:
                            wv_ = slabs[c // 16][:, c % 16, :]
                        kb.mm(pbk[:, :], ogt[:, c, t * 128:(t + 1) * 128], wv_, start=(c == 0), stop=(c == C - 1))
                    kb.copy(yb.k(t, (slice(None), t, slice(cb * 512, (cb + 1) * 512))), pbk[:, :], e=("act" if t % 2 else "dve"))
            for t in range(4):
                i = g * 4 + t
                ht = hb[i % 2]
                rows = (slice(i * 128, (i + 1) * 128), slice(None))
                kb.dma("sp", ht[:, :], src.k(i, rows))
                y = yb.k(t, (slice(None), t, slice(None)))
                kb.act(junk[:, :], y, AF.Square, accum=ss[:, 0:1])
                self.rstd_from_ss(ss[:, 1:2], ss[:, 0:1], D)
                kb.stt(y, y, ss[:, 1:2], postb[:, :], ALU.mult, ALU.mult)
                kb.tt(ht[:, :], ht[:, :], y, ALU.add)
                kb.dma("sp", self.out.k(i, rows), ht[:, :])

    def proj_fm(self, pbk, w, cols, g):
        kb = self.kb
        n = cols.stop - cols.start
        for kc in range(16):
            kb.mm(vv(pbk[:, :], pbk.t[0:n, :]), w[:, kc, cols], self.uT[:, kc, g * 512:(g + 1) * 512],
                  start=(kc == 0), stop=(kc == 15))

    def proj_fm_gen(self, pbk, w, cols, g, step=8):
        kb = self.kb
        n = cols.stop - cols.start
        for kc in range(16):
            kb.mm(vv(pbk[:, :], pbk.t[0:n, :]), w[:, kc, cols], self.uT[:, kc, g * 512:(g + 1) * 512],
                  start=(kc == 0), stop=(kc == 15))
            if kc % step == step - 1:
                yield

    def headnorm_gate_store(self, psO, hn, zs_g, row0, g, tmp, ogb, psN=None):
        kb = self.kb
        sq = tmp["sq"]
        kb.act(sq[:, :], psO, AF.Square)
        if psN is None:
            psN = self.pb[7]
        kb.mm(psN[:, :], self.onesB, sq[:, :])
        rb = tmp["rb"]
        kb.act(rb[:, :], psN[:, :], AF.Ln, bias=self.epsb[:, :], scale=1.0 / 128)
        kb.act(rb[:, :], rb[:, :], AF.Exp, scale=-0.5)
        kb.stt(rb[:, :], psO, hn, rb[:, :], ALU.mult, ALU.mult)
        kb.tt(ogb[:, :], rb[:, :], zs_g, ALU.mult)
        kb.dma("sp", self.og.k(("og", row0, g), (slice(row0, row0 + 128), slice(g * 512, (g + 1) * 512))), ogb[:, :])

    def hgrn(self, l):
        kb = self.kb
        p = "l%d_" % l
        win = self.W[p + "w_in"]
        lbp = kb.sb([128, 64], F32)
        kb.dma("sp", lbp[:, :], self.lb_d[:, :])
        kb.act(lbp[:, :], lbp[:, :], AF.Exp)
        den = kb.sb([128, 16], F32)
        lb = kb.sb([128, 16], F32)
        oml = kb.sb([128, 16], F32)
        kb.tt(den[:, :], lbp[:, 0:16], lbp[:, 16:32], ALU.add)
        kb.tt(den[:, :], den[:, :], lbp[:, 32:48], ALU.add)
        kb.tt(den[:, :], den[:, :], lbp[:, 48:64], ALU.add)
        kb.recip(den[:, :], den[:, :])
        kb.memset(lb[:, :], 0.0)
        for j in range(1, l + 1):
            kb.tt(lb[:, :], lb[:, :], lbp[:, j * 16:(j + 1) * 16], ALU.add)
        kb.tt(lb[:, :], lb[:, :], den[:, :], ALU.mult)
        kb.ts(oml[:, :], lb[:, :], -1.0, 1.0, ALU.mult, ALU.add)
        siga = kb.sb([128, 16], F32)
        sigc = kb.sb([128, 16], F32)
        kb.ts(siga[:, :], oml[:, :], 0.5, None, ALU.mult)
        kb.tt(sigc[:, :], lb[:, :], siga[:, :], ALU.add)
        hn = kb.sb([128, 1], F32)
        kb.dma("sp", hn[:, :], self.W[p + "hn"][:, :])

        wq = [kb.sb([128, 16, 128], BF16) for _ in range(2)]
        wf = [kb.sb([128, 16, 128], BF16) for _ in range(2)]
        wv = [kb.sb([128, 16, 128], BF16) for _ in range(2)]
        wz = [kb.sb([128, 16, 128], BF16) for _ in range(2)]
        t1 = kb.sb([128, S], F32)
        t2 = kb.sb([128, S], F32)
        t3 = kb.sb([128, S], F32)
        t4 = kb.sb([128, S], F32)
        kdt = kb.sb([128, S], BF16)
        qtS = [kb.sb([128, S], BF16) for _ in range(2)]
        ktS = [kb.sb([128, S], BF16) for _ in range(2)]
        zsS = [kb.sb([128, S], BF16) for _ in range(2)]
        vtS = [kb.sb([128, 16, 128], BF16) for _ in range(2)]
        ktokS = [kb.sb([128, 16, 128], BF16) for _ in range(2)]
        elS = [kb.sb([128, 32], F32) for _ in range(2)]
        AT = [kb.sb([128, 128], BF16) for _ in range(2)]
        St = kb.sb([128, 128], F32)
        SbS = [kb.sb([128, 128], BF16) for _ in range(4)]
        tmp = {"sq": kb.sb([128, 512], BF16), "rb": kb.sb([128, 512], F32)}
        ogb = [kb.sb([128, 512], BF16) for _ in range(2)]
        scale = 128 ** -0.5

        def wload(h):
            s_ = h % 2
            for j, wb in enumerate((wq, wf, wv, wz)):
                c0 = j * 2048 + h * 128
                kb.dma("pool", wb[s_][:, :, :], vv(win[:, :], win.t[:, c0:c0 + 128].rearrange("(c p) n -> p c n", p=128)))

        def prep(h):
            s_ = h % 2
            qt, kt, zs, vt, ktok, elb = qtS[s_], ktS[s_], zsS[s_], vtS[s_], ktokS[s_], elS[s_]
            ah = siga[:, h:h + 1]
            ch = sigc[:, h:h + 1]
            for g in range(4):
                gs = slice(g * 512, (g + 1) * 512)
                G_ = lambda b_: b_.k(g, (slice(None), gs))
                pA = self.pb[0]
                yield from self.proj_fm_gen(pA, wf[s_], slice(0, 128), g)
                kb.act(G_(t1), pA[:, :], AF.Tanh, scale=0.5)
                pQ = self.pb[1]
                yield from self.proj_fm_gen(pQ, wq[s_], slice(0, 128), g)
                kb.act(G_(t4), pQ[:, :], AF.Silu)
                kb.ts(G_(t1), G_(t1), ah, ch, ALU.mult, ALU.add)
                pZ = self.pb[2]
                yield from self.proj_fm_gen(pZ, wz[s_], slice(0, 128), g)
                kb.act(G_(zs), pZ[:, :], AF.Silu)
                kb.ts(G_(t2), G_(t1), -1.0, 1.0, ALU.mult, ALU.add)
                yield
            for g in range(4):
                gs = slice(g * 512, (g + 1) * 512)
                G_ = lambda b_: b_.k(g, (slice(None), gs))
                kb.act(G_(t1), G_(t1), AF.Ln)
                yield
                kb.scan(G_(t3), vv(self.cst[:, :], self.cst.t[:, 705 + g * 512:705 + (g + 1) * 512]), G_(t1))
                yield
                kb.act(G_(t1), G_(t3), AF.Exp)
                yield
                kb.stt(G_(qt), G_(t4), scale, G_(t1), ALU.mult, ALU.mult)
                yield
                kb.act(G_(t4), G_(t3), AF.Exp, scale=-1.0)
                yield
                kb.tt(G_(kt), G_(t2), G_(t4), ALU.mult)
                yield
                kb.tt(vv(G_(kdt), kdt.t[:, gs].rearrange("p (a b) -> p a b", b=64)),
                      vv(G_(kt), kt.t[:, gs].rearrange("p (a b) -> p a b", b=64)),
                      vv(G_(t1), t1.t[:, g * 512 + 63:(g + 1) * 512:64].unsqueeze(2).to_broadcast([128, 8, 64])),
                      ALU.mult)
                kb.copy(elb.k(g, (slice(None), slice(g * 8, (g + 1) * 8))), vv(G_(t1), t1.t[:, g * 512 + 63:(g + 1) * 512:64]))
                yield
            for g in range(4):
                pV = self.pb[g % 2]
                for j in range(4):
                    i = g * 4 + j
                    for kc in range(16):
                        kb.mm(pV[:, j * 128:(j + 1) * 128], self.uT[:, kc, i * 128:(i + 1) * 128], wv[s_][:, kc, :],
                              start=(kc == 0), stop=(kc == 15))
                    yield
                kb.copy(vt.k(g, (slice(None), slice(g * 4, g * 4 + 4), slice(None))),
                        vv(pV[:, :], pV.t[:, :].rearrange("p (a b) -> p a b", b=128)), e="act")
                pT = self.pb[2]
                pTb = pT.t[:, :].bitcast(BF16)
                for j in range(4):
                    i = g * 4 + j
                    kb.tr(vv(pT[:, :], pTb[:, j * 128:(j + 1) * 128]), kdt.k(g, (slice(None), slice(i * 128, (i + 1) * 128))), self.identB)
                yield
                kb.copy(ktok.k(g, (slice(None), slice(g * 4, g * 4 + 4), slice(None))),
                        vv(pT[:, :], pTb[:, 0:512].rearrange("p (a b) -> p a b", b=128)))
                yield

        def loop(h):
            s_ = h % 2
            qt, kt, zs, vt, ktok, elb = qtS[s_], ktS[s_], zsS[s_], vtS[s_], ktokS[s_], elS[s_]
            kb.memset(St[:, :], 0.0)
            for n_ in range(4):
                kb.memset(SbS[n_][:, :], 0.0)

            def stage1(i):
                g = i // 4
                TB = self.pb[6 + i % 2]
                ts_ = slice(i * 128, (i + 1) * 128)
                kb.mm(TB[:, 0:128], kt.k(g, (slice(None), ts_)), qt.k(g, (slice(None), ts_)))
                kb.mm(TB[:, 128:256], ktok.k(g, (slice(0, 64), i, slice(None))), vt.k(g, (slice(0, 64), i, slice(None))))
                YB = self.pb[3]
                kb.mm(YB[:, (i % 2) * 128:(i % 2) * 128 + 128], ktok.k(g, (slice(64, 128), i, slice(None))), vt.k(g, (slice(64, 128), i, slice(None))))

            def stage2(i):
                g = i // 4
                TB = self.pb[6 + i % 2]
                kb.tt(AT[i % 2][:, :], TB[:, 0:128], self.maskHG, ALU.mult)
                for c in range(2):
                    n_ = 2 * i + c
                    el = elb.k(g, (slice(None), slice(n_, n_ + 1)))
                    src_ = TB[:, 128:256] if c == 0 else self.pb[3][:, (i % 2) * 128:(i % 2) * 128 + 128]
                    kb.stt(St[:, :], St[:, :], el, src_, ALU.mult, ALU.add)
                    kb.copy(SbS[n_ % 4][:, :], St[:, :], e="pool")

            def stage3(i):
                g = i // 4
                j = i % 4
                pO = self.pb[4 + g % 2]
                kb.mm(pO[:, j * 128:(j + 1) * 128], vt.k(g, (slice(None), i, slice(None))), AT[i % 2][:, :], start=True, stop=False)
                for c in range(2):
                    n_ = 2 * i + c
                    cs = slice(i * 128 + c * 64, i * 128 + (c + 1) * 64)
                    kb.mm(pO[:, j * 128 + c * 64:j * 128 + (c + 1) * 64], SbS[(n_ - 1) % 4][:, :], qt.k(g, (slice(None), cs)),
                          start=False, stop=(c == 1))

            stage1(0)
            yield
            stage2(0)
            yield
            for i in range(16):
                if i + 1 < 16:
                    stage1(i + 1)
                    yield
                stage3(i)
                yield
                if i + 1 < 16:
                    stage2(i + 1)
                    yield
                if i % 4 == 3:
                    g = i // 4
                    self.headnorm_gate_store(self.pb[4 + g % 2][:, :], hn[:, 0:1], zs.k(g, (slice(None), slice(g * 512, (g + 1) * 512))),
                                             h * 128, g, tmp, ogb[g % 2], psN=self.pb[3])
                    yield

        def run_gens(gens):
            alive = [True] * len(gens)
            while any(alive):
                for e_, gn in enumerate(gens):
                    if alive[e_]:
                        try:
                            next(gn)
                        except StopIteration:
                            alive[e_] = False

        wload(0)
        wload(1)
        run_gens([prep(0)])
        for h in range(16):
            if h + 2 < 16:
                wload(h + 2)
            gens = [loop(h)]
            if h + 1 < 16:
                gens.append(prep(h + 1))
            run_gens(gens)

    def rope(self, pin, out, gs, cos2, sin2, tmp):
        kb = self.kb
        tf, ta, tb2 = tmp
        kb.copy(tf[:, :], pin, e="act")
        pR = self.pb[4]
        pr = vv(pR[:, :], pR.t[0:64, :])
        kb.mm(pr, vv(self.cst[:, :], self.cst.t[0:64, 640:704]), tf[:, :])
        kb.tt(ta[:, :], tf[:, :], vv(cos2[:, :], cos2.t[:, gs]), ALU.mult)
        kb.tt(tb2[:, :], pr, vv(sin2[:, :], sin2.t[:, gs]), ALU.mult)
        kb.tt(out, ta[:, :], tb2[:, :], ALU.add)

    def sin_reduced(self, out, ang, t1, ti):
        kb = self.kb
        PI = float(np.pi)
        kb.ts(t1[:, :], ang, 1.0 / (2 * PI), None, ALU.mult)
        kb.copy(ti[:, :], t1[:, :])
        kb.copy(t1[:, :], ti[:, :])
        kb.stt(out, t1[:, :], -2 * PI, ang, ALU.mult, ALU.add)
        kb.ts(t1[:, :], out, PI, None, ALU.is_gt)
        kb.stt(out, t1[:, :], -2 * PI, out, ALU.mult, ALU.add)
        kb.ts(t1[:, :], out, -PI, None, ALU.is_lt)
        kb.stt(out, t1[:, :], 2 * PI, out, ALU.mult, ALU.add)
        kb.ts(out, out, -3.1415925, 3.1415925, ALU.max, ALU.min)
        kb.act(out, out, AF.Sin)

    def mla(self, l):
        kb = self.kb
        p = "l%d_" % l
        win = self.W[p + "w_in"]
        wuq = self.W[p + "w_uq"]
        wukv = self.W[p + "w_ukv"]
        cqn = kb.sb([128, 4, S], BF16)
        ckvn = kb.sb([128, 4, S], BF16)
        krT = kb.sb([64, S], BF16)
        cos2 = kb.sb([64, S], F32)
        sin2 = kb.sb([64, S], F32)
        nw = kb.sb([128, 8], F32)
        kb.dma("sp", nw[:, 0:4], self.W[p + "qn"][:, :])
        kb.dma("sp", nw[:, 4:8], self.W[p + "kvn"][:, :])
        top = kb.es
        with contextlib.ExitStack() as s0:
            kb.es = s0
            posi = kb.sb([64, S], I32)
            ang = kb.sb([64, S], F32)
            t1 = kb.sb([64, S], F32)
            kb.dma("sp", posi[:, :], vv(self.pos[:, :], self.pos.t[0:1, :].partition_broadcast(64).squeeze(1)))
            kb.copy(ang[:, :], posi[:, :])
            kb.ts(ang[:, :], ang[:, :], vv(self.cst[:, :], self.cst.t[0:64, 704:705]), None, ALU.mult)
            self.sin_reduced(sin2[:, :], ang[:, :], t1, posi)
            kb.ts(ang[:, :], ang[:, :], float(np.pi / 2), None, ALU.add)
            self.sin_reduced(cos2[:, :], ang[:, :], t1, posi)
            kb.barrier()
        with contextlib.ExitStack() as s1:
            kb.es = s1
            wc = [kb.sb([128, 16, 512], BF16) for _ in range(2)]
            wkr = kb.sb([128, 16, 64], BF16)
            cf = kb.sb([128, 4, 512], F32)
            sq = kb.sb([128, 512], BF16)
            rb = kb.sb([128, 512], F32)
            rt = [kb.sb([64, 512], F32) for _ in range(3)]
            for j in range(2):
                kb.dma("pool", wc[j][:, :, :], vv(win[:, :], win.t[:, j * 512:(j + 1) * 512].rearrange("(c p) n -> p c n", p=128)))
            kb.dma("pool", wkr[:, :, :], vv(win[:, :], win.t[:, 1024:1088].rearrange("(c p) n -> p c n", p=128)))
            for g in range(4):
                gs = slice(g * 512, (g + 1) * 512)
                for j, dst in enumerate((cqn, ckvn)):
                    psN = self.pb[7]
                    for c in range(4):
                        pA = self.pb[c % 2]
                        self.proj_fm(pA, wc[j], slice(c * 128, (c + 1) * 128), g)
                        kb.copy(cf[:, c, :], pA[:, :], e="act")
                        kb.act(sq[:, :], pA[:, :], AF.Square)
                        kb.mm(psN[:, :], self.onesB, sq[:, :], start=(c == 0), stop=(c == 3))
                    kb.act(rb[:, :], psN[:, :], AF.Ln, bias=self.epsb[:, :], scale=1.0 / 512)
                    kb.act(rb[:, :], rb[:, :], AF.Exp, scale=-0.5)
                    for c in range(4):
                        kb.stt(dst.k(g, (slice(None), c, gs)), cf[:, c, :], nw[:, j * 4 + c:j * 4 + c + 1], rb[:, :], ALU.mult, ALU.mult)
                pA = self.pb[2]
                self.proj_fm(pA, wkr, slice(0, 64), g)
                self.rope(vv(pA[:, :], pA.t[0:64, :]), krT.k(g, (slice(None), gs)), gs, cos2, sin2, rt)
            kb.barrier()
        with contextlib.ExitStack() as s2:
            kb.es = s2
            wq_ = [kb.sb([128, 4, 192], BF16) for _ in range(2)]
            wkv_ = [kb.sb([128, 4, 256], BF16) for _ in range(2)]
            wz = [kb.sb([128, 16, 128], BF16) for _ in range(2)]
            qn = kb.sb([128, S], BF16)
            qr = kb.sb([64, S], BF16)
            kn = kb.sb([128, S], BF16)
            vt = kb.sb([128, 16, 128], BF16)
            zs = kb.sb([128, S], BF16)
            PT = [kb.sb([128, 512], BF16) for _ in range(2)]
            rt = [kb.sb([64, 512], F32) for _ in range(3)]
            rden = kb.sb([128, 512], F32)
            o1 = kb.sb([128, 512], F32)
            ogb = [kb.sb([128, 512], BF16) for _ in range(2)]
            scale = 192 ** -0.5

            def wload(h):
                s = h % 2
                kb.dma("pool", wq_[s][:, :, :], vv(wuq[:, :], wuq.t[:, h * 192:(h + 1) * 192].rearrange("(c p) n -> p c n", p=128)))
                kb.dma("pool", wkv_[s][:, :, :], vv(wukv[:, :], wukv.t[:, h * 256:(h + 1) * 256].rearrange("(c p) n -> p c n", p=128)))
                c0 = 1088 + h * 128
                kb.dma("pool", wz[s][:, :, :], vv(win[:, :], win.t[:, c0:c0 + 128].rearrange("(c p) n -> p c n", p=128)))

            wload(0)
            for h in range(16):
                s = h % 2
                if h + 1 < 16:
                    wload(h + 1)
                for g in range(4):
                    gs = slice(g * 512, (g + 1) * 512)
                    pA = self.pb[0]
                    for kc in range(4):
                        kb.mm(pA[:, :], wq_[s][:, kc, 0:128], cqn.k(g, (slice(None), kc, gs)), start=(kc == 0), stop=(kc == 3))
                    kb.copy(qn.k(g, (slice(None), gs)), pA[:, :], e="act")
                    pB = self.pb[1]
                    pb64 = vv(pB[:, :], pB.t[0:64, :])
                    for kc in range(4):
                        kb.mm(pb64, wq_[s][:, kc, 128:192], cqn.k(g, (slice(None), kc, gs)), start=(kc == 0), stop=(kc == 3))
                    self.rope(pb64, qr.k(g, (slice(None), gs)), gs, cos2, sin2, rt)
                    pA = self.pb[0]
                    for kc in range(4):
                        kb.mm(pA[:, :], wkv_[s][:, kc, 0:128], ckvn.k(g, (slice(None), kc, gs)), start=(kc == 0), stop=(kc == 3))
                    kb.copy(kn.k(g, (slice(None), gs)), pA[:, :])
                    pV = self.pb[1]
                    for j in range(4):
                        i = g * 4 + j
                        for kc in range(4):
                            kb.mm(pV[:, j * 128:(j + 1) * 128], ckvn.k(g, (slice(None), kc, slice(i * 128, (i + 1) * 128))),
                                  wkv_[s][:, kc, 128:256], start=(kc == 0), stop=(kc == 3))
                    kb.copy(vt.k(g, (slice(None), slice(g * 4, g * 4 + 4), slice(None))),
                            vv(pV[:, :], pV.t[:, :].rearrange("p (a b) -> p a b", b=128)), e="act")
                    pZ = self.pb[0]
                    self.proj_fm(pZ, wz[s], slice(0, 128), g)
                    kb.act(zs.k(g, (slice(None), gs)), pZ[:, :], AF.Silu)
                seq = [(g, j) for g in range(4) for j in range(4 * g + 4)]

                def emit_qk(idx):
                    g, j = seq[idx]
                    c0 = max(0, j * 128 - g * 512)
                    ks = slice(j * 128, (j + 1) * 128)
                    qs = slice(g * 512 + c0, (g + 1) * 512)
                    gk = j // 4
                    pS = self.pb[2 + idx % 2]
                    kb.mm(pS[:, c0:512], kn.k(gk, (slice(None), ks)), qn.k(g, (slice(None), qs)), start=True, stop=False)
                    kb.mm(pS[:, c0:512], krT.k(gk, (slice(None), ks)), qr.k(g, (slice(None), qs)), start=False, stop=True)

                def emit_rest(idx):
                    g, j = seq[idx]
                    gs = slice(g * 512, (g + 1) * 512)
                    nj = 4 * g + 4
                    c0 = max(0, j * 128 - g * 512)
                    gk = j // 4
                    pS = self.pb[2 + idx % 2]
                    pt = PT[idx % 2]
                    pO = self.pb[4 + g % 2]
                    pD = self.pb[6 + g % 2]
                    kb.act(pt[:, c0:512], pS[:, c0:512], AF.Exp, scale=scale)
                    if j >= 4 * g:
                        kb.memset(vv(pt[:, :], pt.t[64:128, c0:c0 + 64]), 0.0)
                    kb.mm(pO[:, c0:512], vt.k(gk, (slice(None), j, slice(None))), pt[:, c0:512], start=(j == 0), stop=(j == nj - 1))
                    kb.mm(pD[:, c0:512], self.onesB, pt[:, c0:512], start=(j == 0), stop=(j == nj - 1))
                    if j == nj - 1:
                        kb.act(rden[:, :], pD[:, :], AF.Ln)
                        kb.act(rden[:, :], rden[:, :], AF.Exp, scale=-1.0)
                        kb.tt(o1[:, :], pO[:, :], rden[:, :], ALU.mult)
                        ob = ogb[g % 2]
                        kb.tt(ob[:, :], o1[:, :], zs.k(g, (slice(None), gs)), ALU.mult)
                        kb.dma("sp", self.og.k(("og", h * 128, g), (slice(h * 128, h * 128 + 128), gs)), ob[:, :])

                emit_qk(0)
                for idx in range(len(seq)):
                    if idx + 1 < len(seq):
                        emit_qk(idx + 1)
                    emit_rest(idx)
            kb.barrier()
        kb.es = top

    def gdn(self, l):
        kb = self.kb
        p = "l%d_" % l
        win = self.W[p + "w_in"]
        top = kb.es
        cw = kb.sb([128, 256], F32)
        kb.dma("sp", cw[:, :], self.W[p + "conv"][:, :])
        hn = kb.sb([128, 1], F32)
        kb.dma("sp", hn[:, :], self.W[p + "hn"][:, :])
        gcT = kb.sb([32, S], F32)
        egcT = kb.sb([32, S], F32)
        egc_tok = kb.sb([128, 16, 32], F32)
        edl_tok = kb.sb([128, 16, 32], F32)
        beta_tok = kb.sb([128, 16, 32], F32)
        nbeta_tok = kb.sb([128, 16, 32], F32)
        ngc_tok = kb.sb([128, 16, 32], F32)
        NEGHG = kb.sb([128, 128], BF16)
        kb.ts(NEGHG[:, :], self.maskHG, 1e6, -1e6, ALU.mult, ALU.add)
        Up = self.cst[:, 512:640]
        id32 = vv(self.cst[:, :], self.cst.t[0:32, 0:32])

        def wslab(dst, c0, n=128):
            kb.dma("pool", dst[:, :, :], vv(win[:, :], win.t[:, c0:c0 + n].rearrange("(c p) n -> p c n", p=128)))

        with contextlib.ExitStack() as s0:
            kb.es = s0
            wa = kb.sb([128, 16, 32], BF16)
            wb = kb.sb([128, 16, 32], BF16)
            gT = kb.sb([32, S], F32)
            bT = kb.sb([32, S], F32)
            prm = kb.sb([32, 4], F32)
            gtk = kb.sb([128, 64], F32)
            wslab(wa, 12288, 32)
            wslab(wb, 12320, 32)
            kb.dma("sp", prm[:, 0:1], self.W[p + "alog"][:, :])
            kb.dma("sp", prm[:, 1:2], self.W[p + "dtb"][:, :])
            kb.act(prm[:, 2:3], prm[:, 0:1], AF.Exp)
            kb.ts(prm[:, 2:3], prm[:, 2:3], -1.0, None, ALU.mult)
            for g in range(4):
                gs = slice(g * 512, (g + 1) * 512)
                pA = self.pb[0]
                self.proj_fm(pA, wa, slice(0, 32), g)
                pa32 = vv(pA[:, :], pA.t[0:32, :])
                kb.act(gT[:, gs], pa32, AF.Exp, bias=prm[:, 1:2])
                kb.act(gT[:, gs], gT[:, gs], AF.Ln, bias=vv(self.onesF1[:, :], self.onesF1.t[0:32, :]))
                kb.ts(gT[:, gs], gT[:, gs], prm[:, 2:3], None, ALU.mult)
                pB = self.pb[1]
                self.proj_fm(pB, wb, slice(0, 32), g)
                kb.act(bT[:, gs], vv(pB[:, :], pB.t[0:32, :]), AF.Sigmoid)
            kb.scan(gcT[:, :], vv(self.cst[:, :], self.cst.t[0:32, 705:705 + S]), gT[:, :])
            kb.act(egcT[:, :], gcT[:, :], AF.Exp)
            for i in range(16):
                tl = slice(i * 128, (i + 1) * 128)
                pT = self.pb[2]
                kb.tr(pT[:, 0:32], gT[:, tl], id32)
                kb.tr(pT[:, 32:64], bT[:, tl], id32)
                kb.copy(gtk[:, :], pT[:, 0:64], e="act")
                kb.copy(beta_tok[:, i, :], gtk[:, 32:64])
                kb.ts(nbeta_tok[:, i, :], gtk[:, 32:64], -1.0, None, ALU.mult)
                pG = self.pb[3]
                kb.mm(pG[:, 0:32], self.maskHG, gtk[:, 0:32])
                kb.mm(pG[:, 32:64], Up, gtk[:, 0:32])
                kb.ts(ngc_tok[:, i, :], pG[:, 0:32], -1.0, None, ALU.mult)
                kb.act(egc_tok[:, i, :], pG[:, 0:32], AF.Exp)
                kb.act(edl_tok[:, i, :], pG[:, 32:64], AF.Exp)
                kb.tt(edl_tok[:, i, :], edl_tok[:, i, :], gtk[:, 32:64], ALU.mult)
            kb.barrier()
        kb.es = top
        wq = kb.sb([128, 16, 128], BF16)
        wk = kb.sb([128, 16, 128], BF16)
        wvz = [kb.sb([128, 16, 128], BF16) for _ in range(2)]
        xp = kb.sb([128, 3 + S], F32)
        xs = kb.sb([128, S], F32)
        knb = kb.sb([128, S], BF16)
        qnb = kb.sb([128, S], BF16)
        kb.memset(xp[:, 0:3], 0.0)

        def r32(v):
            return vv(v, v.ap.bitcast(F32R))

        HB = []
        for e in range(2):
            b = {}
            b["vT"] = kb.sb([128, S], BF16)
            b["zs"] = kb.sb([128, S], BF16)
            b["qtil"] = kb.sb([128, S], BF16)
            b["egl"] = kb.sb([128, 32], F32)
            b["selh"] = kb.sb([32, 128], F32)
            b["tmp"] = {"sq": kb.sb([128, 512], BF16), "rb": kb.sb([128, 512], F32)}
            b["ogb"] = [kb.sb([128, 512], BF16) for _ in range(1)]
            b["dm"] = kb.sb([128, 128], F32)
            b["DT"] = kb.sb([128, 128], F32)
            b["t1"] = b["dm"]
            b["attnT"] = kb.sb([128, 128], BF16)
            b["PTb"] = [kb.sb([128, 128], F32) for _ in range(6)]
            b["W"] = [kb.sb([128, 384], F32) for _ in range(2)]
            b["kd"] = kb.sb([128, 128], BF16)
            b["wT"] = kb.sb([128, 128], BF16)
            b["vnew"] = kb.sb([128, 128], BF16)
            b["St"] = kb.sb([128, 128], F32)
            b["Sb"] = kb.sb([128, 128], BF16)
            b["bank"] = self.pb[4 * e:4 * e + 4]
            HB.append(b)
        tmpq = HB[0]["tmp"]

        def conv_silu(w, blk, dst):
            for g in range(4):
                pA = self.pb[g % 2]
                self.proj_fm(pA, w, slice(0, 128), g)
                kb.copy(xp.k(g, (slice(None), slice(3 + g * 512, 3 + (g + 1) * 512))), pA[:, :], e="act")
            kb.ts(xs[:, :], xp[:, 3:3 + S], cw[:, blk * 4 + 3:blk * 4 + 4], None, ALU.mult)
            for j in (2, 1, 0):
                kb.stt(xs[:, :], xp[:, j:j + S], cw[:, blk * 4 + j:blk * 4 + j + 1], xs[:, :], ALU.mult, ALU.add)
            kb.act(dst, xs[:, :], AF.Silu)

        def l2n(dst, sc):
            for g in range(4):
                gs = slice(g * 512, (g + 1) * 512)
                sq = HB[g % 2]["tmp"]["sq"]
                rb = HB[g % 2]["tmp"]["rb"]
                kb.act(sq[:, :], xs[:, gs], AF.Square)
                psN = self.pb[7]
                kb.mm(psN[:, :], self.onesB, sq[:, :])
                kb.act(rb[:, :], psN[:, :], AF.Ln, bias=self.epsb[:, :], scale=1.0)
                kb.act(rb[:, :], rb[:, :], AF.Exp, scale=-0.5)
                kb.stt(dst[:, gs], xs[:, gs], sc, rb[:, :], ALU.mult, ALU.mult)

        def head_prep(hv, e, b):
            wslab(wvz[0], 4096 + hv * 128)
            wslab(wvz[1], 8192 + hv * 128)
            conv_silu(wvz[0], 32 + hv, b["vT"][:, :])
            kb.copy(b["selh"][:, :], vv(self.cst[:, :], self.cst.t[0:32, hv:hv + 1].to_broadcast([32, 128])))
            for g in range(4):
                gs = slice(g * 512, (g + 1) * 512)
                pZ = self.pb[g % 2]
                self.proj_fm(pZ, wvz[1], slice(0, 128), g)
                kb.act(b["zs"][:, gs], pZ[:, :], AF.Silu)
                pE = self.pb[2 + g % 2]
                kb.mm(pE[:, :], b["selh"][:, :], egcT[:, gs])
                kb.tt(b["qtil"][:, gs], qnb[:, gs], pE[:, :], ALU.mult)
                kb.copy(b["egl"][:, g * 8:(g + 1) * 8], vv(pE[:, :], pE.t[:, 63:512:64]), e="act")
            kb.memset(b["St"][:, :], 0.0)
            kb.memset(b["Sb"][:, :], 0.0)

        def head_tiles(hv, e, b):
            B0, B1, B2, B3 = b["bank"]
            dm, DT, t1, attnT, PTb = b["dm"], b["DT"], b["t1"], b["attnT"], b["PTb"]
            kd, wT, vnew, St, Sb = b["kd"], b["wT"], b["vnew"], b["St"], b["Sb"]
            hcol = slice(hv, hv + 1)
            for g in range(4):
                gs = slice(g * 512, (g + 1) * 512)
                pO = B3
                for j in range(4):
                    i = g * 4 + j
                    tl = slice(i * 128, (i + 1) * 128)
                    r_gc = B0[:, 0:128]
                    r_G = B0[:, 128:256]
                    r_KQ = B0[:, 256:384]
                    kb.mm(r_gc, b["selh"][:, :], gcT[:, tl], start=True, stop=False)
                    kb.mm(r_gc, self.identB, NEGHG[:, :], start=False, stop=True)
                    kb.mm(r_G, knb[:, tl], knb[:, tl])
                    kb.mm(r_KQ, knb[:, tl], qnb[:, tl])
                    yield
                    kb.act(DT[:, :], r_gc, AF.Exp, bias=ngc_tok[:, i, hcol])
                    yield
                    kb.stt(t1[:, :], r_G, nbeta_tok[:, i, hcol], DT[:, :], ALU.mult, ALU.mult)
                    kb.tt(r32(PTb[0][:, :]), t1[:, :], self.maskSU, ALU.mult)
                    kb.stt(attnT[:, :], r_KQ, beta_tok[:, i, hcol], DT[:, :], ALU.mult, ALU.mult)
                    yield
                    b2b = B2.t[:, :].bitcast(BF16)
                    r_N = B2[:, 0:128]
                    r_vt = V(b2b[:, 256:384], B2, None)
                    r_kt = V(b2b[:, 384:512], B2, None)
                    r_app = B2[:, 0:256]
                    r_wT = B2[:, 256:384]
                    r_st = B2[:, 384:512]
                    kb.tr(r_N, PTb[0][:, :], self.identF)
                    kb.tr(r_kt, knb[:, tl], self.identB)
                    kb.tr(r_vt, b["vT"][:, tl], self.identB)
                    yield
                    W = b["W"]
                    kb.copy(r32(W[0][:, 256:384]), r_N, e="act")
                    kb.copy(r32(W[0][:, 0:128]), r_vt, e="act")
                    kb.ts(r32(W[0][:, 128:256]), r_kt, egc_tok[:, i, hcol], None, ALU.mult)
                    kb.ts(kd[:, :], r_kt, edl_tok[:, i, hcol], None, ALU.mult)
                    yield
                    PT = PTb[0]
                    cur = 0
                    r_pt = B2[:, 0:128]
                    for lvl in range(1, 6):
                        if lvl <= 4:
                            kb.mm(B1[:, 0:384], r32(PT[:, :]), r32(W[cur][:, 0:384]))
                        else:
                            kb.mm(B1[:, 0:256], r32(PT[:, :]), r32(W[cur][:, 0:256]))
                        kb.mm(r_pt, r32(W[cur][:, 256:384]), r32(PT[:, :]))
                        yield
                        kb.tt(r32(W[1 - cur][:, 0:256]), W[cur][:, 0:256], B1[:, 0:256], ALU.add)
                        if lvl <= 4:
                            kb.copy(r32(W[1 - cur][:, 256:384]), B1[:, 256:384])
                        kb.copy(r32(PTb[lvl][:, :]), r_pt, e="act")
                        cur = 1 - cur
                        PT = PTb[lvl]
                        yield
                    kb.mm(B1[:, 0:256], r32(PTb[5][:, :]), r32(W[cur][:, 0:256]))
                    yield
                    kb.tt(r32(W[1 - cur][:, 0:256]), W[cur][:, 0:256], B1[:, 0:256], ALU.add)
                    cur = 1 - cur
                    uw = W[cur]
                    yield
                    kb.tr(r_wT, uw[:, 128:256], self.identF)
                    yield
                    kb.copy(wT[:, :], r_wT, e="act")
                    yield
                    for c in range(2):
                        ps_ = slice(c * 64, (c + 1) * 64)
                        r_w = vv(B0[:, :], B0.t[ps_, 384:512])
                        kb.mm(r_w, wT[:, ps_], Sb[:, :])
                        yield
                        kb.tt(vv(vnew[:, :], vnew.t[ps_, :]), vv(uw[:, :], uw.t[ps_, 0:128]), r_w, ALU.subtract)
                        yield
                        oc = pO[:, j * 128 + c * 64:j * 128 + (c + 1) * 64]
                        kb.mm(oc, vv(vnew[:, :], vnew.t[ps_, :]), vv(attnT[:, :], attnT.t[ps_, ps_]), start=True, stop=False)
                        kb.mm(oc, Sb[:, :], b["qtil"][:, i * 128 + c * 64:i * 128 + (c + 1) * 64], start=False, stop=True)
                        kb.mm(r_st, vv(kd[:, :], kd.t[ps_, :]), vv(vnew[:, :], vnew.t[ps_, :]))
                        yield
                        egl = b["egl"][:, i * 2 + c:i * 2 + c + 1]
                        kb.stt(St[:, :], St[:, :], egl, r_st, ALU.mult, ALU.add)
                        yield
                        kb.copy(Sb[:, :], St[:, :], e="act")
                        yield
                sq = b["tmp"]["sq"]
                rb = b["tmp"]["rb"]
                ob = b["ogb"][0]
                kb.act(sq[:, :], pO[:, :], AF.Square)
                yield
                kb.mm(B1[:, :], self.onesB, sq[:, :])
                yield
                kb.act(rb[:, :], B1[:, :], AF.Ln, bias=self.epsb[:, :], scale=1.0 / 128)
                yield
                kb.act(rb[:, :], rb[:, :], AF.Exp, scale=-0.5)
                yield
                kb.stt(rb[:, :], pO[:, :], hn[:, 0:1], rb[:, :], ALU.mult, ALU.mult)
                yield
                kb.tt(ob[:, :], rb[:, :], b["zs"][:, gs], ALU.mult)
                kb.dma("sp", self.og.k(("og", hv * 128, g), (slice(hv * 128, hv * 128 + 128), gs)), ob[:, :])
                yield

        for hq in range(16):
            wslab(wq, hq * 128)
            wslab(wk, 2048 + hq * 128)
            conv_silu(wq, hq, xs[:, :])
            l2n(qnb, 128 ** -0.5)
            conv_silu(wk, 16 + hq, xs[:, :])
            l2n(knb, 1.0)
            gens = []
            for e in range(2):
                head_prep(2 * hq + e, e, HB[e])
            for e in range(2):
                gens.append(head_tiles(2 * hq + e, e, HB[e]))
            alive = [True, True]
            while any(alive):
                for e in range(2):
                    if alive[e]:
                        try:
                            next(gens[e])
                        except StopIteration:
                            alive[e] = False


def prep_inputs(inputs, b, layers):
    f = np.ascontiguousarray
    consts, sel = make_consts()
    m = {"x": f(inputs["x"][b]), "pos": f(inputs["positions"][b:b + 1].astype(np.int32)), "consts": consts, "sel": sel}
    lb = np.asarray(inputs["hgrn_lb"], np.float32)
    m["lbp"] = f(lb.reshape(4, 16, 128).transpose(2, 0, 1).reshape(128, 64))
    for l in range(4):
        p = "l%d_" % l
        m[p + "pre"] = f(np.asarray(inputs[p + "pre_norm"], np.float32).reshape(16, 128).T)
        m[p + "post"] = f(np.asarray(inputs[p + "post_norm"], np.float32).reshape(1, D))
        kind = LAYER_KIND[l]
        m[p + "w_in"] = f(inputs[p + "w_in"])
        m[p + "w_out"] = f(inputs[p + "w_out"])
        if kind == 0:
            m[p + "hn"] = f(np.asarray(inputs[p + "head_norm"], np.float32).reshape(128, 1))
        elif kind == 1:
            m[p + "qn"] = f(np.asarray(inputs[p + "q_norm"], np.float32).reshape(4, 128).T)
            m[p + "kvn"] = f(np.asarray(inputs[p + "kv_norm"], np.float32).reshape(4, 128).T)
            m[p + "w_uq"] = f(inputs[p + "w_uq"])
            m[p + "w_ukv"] = f(inputs[p + "w_ukv"])
        else:
            cw = np.asarray(inputs[p + "conv_w"], np.float32)
            m[p + "conv"] = f(cw.reshape(4, 64, 128).transpose(2, 1, 0).reshape(128, 256))
            m[p + "alog"] = f(np.asarray(inputs[p + "a_log"], np.float32).reshape(32, 1))
            m[p + "dtb"] = f(np.asarray(inputs[p + "dt_bias"], np.float32).reshape(32, 1))
            m[p + "hn"] = f(np.asarray(inputs[p + "head_norm"], np.float32).reshape(128, 1))
    return m


_PROG = {}


def run(inputs, layers=(0, 1, 2, 3), cores=8):
    key = tuple(layers)
    if key not in _PROG:
        _PROG[key] = Prog(list(layers))
    prog = _PROG[key]
    in_maps = [prep_inputs(inputs, b, layers) for b in range(cores)]
    res = run_bass_kernel_spmd(prog.nc, in_maps, core_ids=list(range(cores)))
    return np.stack([np.asarray(r["out"], np.float32) for r in res.results], axis=0)


def kernel(x, positions, hgrn_lb,
           l0_pre_norm, l0_post_norm, l0_w_in, l0_head_norm, l0_w_out,
           l1_pre_norm, l1_post_norm, l1_w_in, l1_q_norm, l1_kv_norm, l1_w_uq, l1_w_ukv, l1_w_out,
           l2_pre_norm, l2_post_norm, l2_w_in, l2_conv_w, l2_a_log, l2_dt_bias, l2_head_norm, l2_w_out,
           l3_pre_norm, l3_post_norm, l3_w_in, l3_head_norm, l3_w_out):
    inputs = dict(
        x=x, positions=positions, hgrn_lb=hgrn_lb,
        l0_pre_norm=l0_pre_norm, l0_post_norm=l0_post_norm, l0_w_in=l0_w_in, l0_head_norm=l0_head_norm, l0_w_out=l0_w_out,
        l1_pre_norm=l1_pre_norm, l1_post_norm=l1_post_norm, l1_w_in=l1_w_in, l1_q_norm=l1_q_norm, l1_kv_norm=l1_kv_norm,
        l1_w_uq=l1_w_uq, l1_w_ukv=l1_w_ukv, l1_w_out=l1_w_out,
        l2_pre_norm=l2_pre_norm, l2_post_norm=l2_post_norm, l2_w_in=l2_w_in, l2_conv_w=l2_conv_w, l2_a_log=l2_a_log,
        l2_dt_bias=l2_dt_bias, l2_head_norm=l2_head_norm, l2_w_out=l2_w_out,
        l3_pre_norm=l3_pre_norm, l3_post_norm=l3_post_norm, l3_w_in=l3_w_in, l3_head_norm=l3_head_norm, l3_w_out=l3_w_out)
    inputs = {k: np.asarray(v) for k, v in inputs.items()}
    return run(inputs)
```

```python
import contextlib
import numpy as np
import concourse.bass as bass
import concourse.mybir as mybir
from concourse.bass_utils import run_bass_kernel_spmd

F32 = mybir.dt.float32
F32R = mybir.dt.float32r
BF16 = mybir.dt.bfloat16
I32 = mybir.dt.int32
AF = mybir.ActivationFunctionType
ALU = mybir.AluOpType

S = 2048
D = 2048
EPS = 1e-6
NDMA = 8
SAME_ENGINE_SYNC = True


class Reg:
    __slots__ = ("w", "r")

    def __init__(self):
        self.w = None
        self.r = {}


class V:
    __slots__ = ("ap", "buf", "key")

    def __init__(self, ap, buf, key):
        self.ap = ap
        self.buf = buf
        self.key = key


class Buf:
    def __init__(self, t, excl=False):
        self.t = t
        self.regs = {}
        self.whole = Reg()
        self.excl = excl

    def __getitem__(self, idx):
        return V(self.t[idx], self, None)

    def k(self, key, idx):
        return V(self.t[idx], self, None if self.excl else key)

    def reg(self, key):
        if key not in self.regs:
            self.regs[key] = Reg()
        return self.regs[key]


def vv(v, ap):
    return V(ap, v.buf, v.key)


class KB:
    def __init__(self, nc, es):
        self.nc = nc
        self.es = es
        self.eng = {"pe": nc.tensor, "act": nc.scalar, "dve": nc.vector, "pool": nc.gpsimd, "sp": nc.sync}
        self.sem = {}
        self.cnt = {}
        self.epoch = {}
        self.seen = {e: {} for e in self.eng}
        self.top_es = es
        for e in self.eng:
            self.epoch[e] = 0
            self.sem[("E", e, 0)] = es.enter_context(nc.semaphore("s_%s_0" % e))
            self.cnt[e] = 0
        self.dcnt = {"sp": 0, "pool": 0}
        for q in ("sp", "pool"):
            for k in range(NDMA):
                self.sem[("D", q, k)] = es.enter_context(nc.semaphore("d_%s%d" % (q, k)))
        self.nbuf = 0

    def sb(self, shape, dt, name=None):
        self.nbuf += 1
        return Buf(self.es.enter_context(self.nc.sbuf_tensor(name or "b%d" % self.nbuf, shape, dt)))

    def ps(self, shape, dt, name=None):
        self.nbuf += 1
        return Buf(self.es.enter_context(self.nc.psum_tensor(name or "p%d" % self.nbuf, shape, dt)), excl=True)

    def dram(self, t):
        return Buf(t)

    def _deps(self, reads, writes):
        deps = {}

        def add(tok):
            if tok is None:
                return
            k, n = tok
            if deps.get(k, 0) < n:
                deps[k] = n

        for v in reads:
            b = v.buf
            add(b.whole.w)
            if v.key is None:
                for r in b.regs.values():
                    add(r.w)
            else:
                add(b.reg(v.key).w)
            if b.excl:
                for k, n in b.whole.r.items():
                    add((k, n))
        for v in writes:
            b = v.buf
            regs = [b.whole] + (list(b.regs.values()) if v.key is None else [b.reg(v.key)])
            for r in regs:
                add(r.w)
                for k, n in r.r.items():
                    add((k, n))
        return deps

    def _wait(self, e, deps):
        for k, n in deps.items():
            if k[0] == "E" and k[1] == e and not (SAME_ENGINE_SYNC and e in ("act", "dve", "pool")):
                continue
            if self.seen[e].get(k, 0) >= n:
                continue
            self.eng[e].wait_ge(self.sem[k], n)
            self.seen[e][k] = n

    def _record(self, tok, reads, writes):
        k, n = tok
        for v in reads:
            r = v.buf.whole if v.key is None else v.buf.reg(v.key)
            if r.r.get(k, 0) < n:
                r.r[k] = n
        for v in writes:
            r = v.buf.whole if v.key is None else v.buf.reg(v.key)
            r.w = tok
            r.r = {}

    def op(self, e, fn, reads, writes):
        reads = [v for v in reads if isinstance(v, V)]
        self._wait(e, self._deps(reads, writes))
        inst = fn()
        if self.cnt[e] >= 30000:
            self.epoch[e] += 1
            self.cnt[e] = 0
            self.sem[("E", e, self.epoch[e])] = self.top_es.enter_context(
                self.nc.semaphore("s_%s_%d" % (e, self.epoch[e])))
        self.cnt[e] += 1
        key = ("E", e, self.epoch[e])
        inst.then_inc(self.sem[key], 1)
        self._record((key, self.cnt[e]), reads, writes)

    def barrier(self):
        for e in self.eng:
            deps = {}
            for o in self.eng:
                if o != e and self.cnt[o] > 0:
                    deps[("E", o, self.epoch[o])] = self.cnt[o]
            self._wait(e, deps)
            self.wait_all(e)

    def dma(self, q, out, in_, **kw):
        deps = self._deps([in_], [out])
        i = self.dcnt[q]
        self.dcnt[q] += 1
        k = i % NDMA
        key = ("D", q, k)
        if i >= NDMA:
            deps[key] = max(deps.get(key, 0), 16 * (i // NDMA))
        self._wait(q, deps)
        inst = self.eng[q].dma_start(out=out.ap, in_=in_.ap, **kw)
        inst.then_inc(self.sem[key], 16)
        self._record((key, 16 * (i // NDMA + 1)), [in_], [out])

    def wait_all(self, e):
        deps = {}
        for q in ("sp", "pool"):
            i = self.dcnt[q]
            for k in range(NDMA):
                n = (i - k + NDMA - 1) // NDMA if i > k else 0
                if n > 0:
                    deps[("D", q, k)] = 16 * n
        self._wait(e, deps)

    def mm(self, out, lhsT, rhs, start=True, stop=True):
        self.op("pe", lambda: self.nc.tensor.matmul(out.ap, lhsT=lhsT.ap, rhs=rhs.ap, start=start, stop=stop),
                [lhsT, rhs], [out])

    def tr(self, out, in_, ident):
        self.op("pe", lambda: self.nc.tensor.transpose(out.ap, in_.ap, ident.ap), [in_, ident], [out])

    def act(self, out, in_, func, bias=None, scale=None, accum=None):
        kw = {}
        rd = [in_]
        if bias is not None:
            kw["bias"] = bias.ap if isinstance(bias, V) else bias
            rd.append(bias)
        if scale is not None:
            kw["scale"] = scale.ap if isinstance(scale, V) else scale
            rd.append(scale)
        wr = [out]
        if accum is not None:
            kw["accum_out"] = accum.ap
            wr.append(accum)
        self.op("act", lambda: self.nc.scalar.activation(out=out.ap, in_=in_.ap, func=func, **kw), rd, wr)

    def _e(self, e):
        return self.nc.vector if e == "dve" else self.nc.gpsimd

    def tt(self, out, a, b, op, e="dve"):
        self.op(e, lambda: self._e(e).tensor_tensor(out=out.ap, in0=a.ap, in1=b.ap, op=op), [a, b], [out])

    def ts(self, out, a, s1, s2, op0, op1=None, e="dve"):
        g = lambda s: s.ap if isinstance(s, V) else s
        if op1 is None:
            f = lambda: self._e(e).tensor_scalar(out=out.ap, in0=a.ap, scalar1=g(s1), scalar2=None, op0=op0)
        else:
            f = lambda: self._e(e).tensor_scalar(out=out.ap, in0=a.ap, scalar1=g(s1), scalar2=g(s2), op0=op0, op1=op1)
        self.op(e, f, [a, s1, s2], [out])

    def stt(self, out, a, s, b, op0, op1):
        g = s.ap if isinstance(s, V) else s
        self.op("dve", lambda: self.nc.vector.scalar_tensor_tensor(out=out.ap, in0=a.ap, scalar=g, in1=b.ap, op0=op0, op1=op1),
                [a, s, b], [out])

    def copy(self, out, a, e="dve"):
        if e == "act":
            self.op("act", lambda: self.nc.scalar.copy(out=out.ap, in_=a.ap), [a], [out])
        else:
            self.op(e, lambda: self._e(e).tensor_copy(out=out.ap, in_=a.ap), [a], [out])

    def memset(self, out, val, e="dve"):
        self.op(e, lambda: self._e(e).memset(out.ap, val), [], [out])

    def recip(self, out, a):
        self.op("dve", lambda: self.nc.vector.reciprocal(out=out.ap, in_=a.ap), [a], [out])

    def recip_fast(self, out, a):
        self.op("dve", lambda: self.nc.vector.reciprocal_approx_fast(out=out.ap, in_=a.ap), [a], [out])

    def scan(self, out, d0, d1, init=0.0):
        self.op("dve", lambda: self.nc.vector.tensor_tensor_scan(out=out.ap, data0=d0.ap, data1=d1.ap, initial=init,
                                                                  op0=ALU.mult, op1=ALU.add), [d0, d1], [out])


NCONST = 705 + 2048


def make_consts():
    c = np.zeros((128, NCONST), np.float32)
    i = np.arange(128)
    c[:, 0:128] = np.eye(128)
    same = (i[:, None] // 64) == (i[None, :] // 64)
    c[:, 128:256] = same & (i[:, None] <= i[None, :])
    c[:, 256:384] = same & (i[:, None] < i[None, :])
    c[:, 384:512] = 1.0
    c[:, 512:640] = same & (i[:, None] > i[None, :])
    R = np.zeros((64, 64), np.float32)
    for a in range(32):
        R[a, 32 + a] = -1.0
        R[32 + a, a] = 1.0
    c[0:64, 640:704] = R.T
    inv_freq = 10000.0 ** (-np.arange(0, 64, 2, dtype=np.float32) / 64)
    c[0:64, 704] = np.concatenate([inv_freq, inv_freq])
    m = np.ones(2048, np.float32)
    m[0::64] = 0.0
    c[:, 705:705 + 2048] = m[None, :]
    sel = np.zeros((32, 32 * 128), np.float32)
    for h in range(32):
        sel[h, h * 128:(h + 1) * 128] = 1.0
    return c, sel


LAYER_KIND = [0, 1, 2, 0]


class Prog:
    def __init__(self, layers):
        self.layers = layers
        nc = bass.Bass("TRN2", target_bir_lowering=False)
        self.nc = nc
        self.es = contextlib.ExitStack()
        self.build()

    def din(self, name, shape, dt=F32):
        return self.kb.dram(self.nc.dram_tensor(name, list(shape), dt, kind="ExternalInput").ap())

    def build(self):
        nc = self.nc
        with self.es as es:
            kb = self.kb = KB(nc, es)
            self.x = self.din("x", [S, D])
            self.pos = self.din("pos", [1, S], I32)
            self.cst_d = self.din("consts", [128, NCONST])
            self.sel_d = self.din("sel", [32, 4096])
            self.lb_d = self.din("lbp", [128, 64])
            self.W = {}
            for l in range(4):
                p = "l%d_" % l
                self.W[p + "pre"] = self.din(p + "pre", [128, 16])
                self.W[p + "post"] = self.din(p + "post", [1, D])
                kind = LAYER_KIND[l]
                if kind == 0:
                    self.W[p + "w_in"] = self.din(p + "w_in", [D, 8192])
                    self.W[p + "hn"] = self.din(p + "hn", [128, 1])
                    self.W[p + "w_out"] = self.din(p + "w_out", [2048, D])
                elif kind == 1:
                    self.W[p + "w_in"] = self.din(p + "w_in", [D, 3136])
                    self.W[p + "qn"] = self.din(p + "qn", [128, 4])
                    self.W[p + "kvn"] = self.din(p + "kvn", [128, 4])
                    self.W[p + "w_uq"] = self.din(p + "w_uq", [512, 3072])
                    self.W[p + "w_ukv"] = self.din(p + "w_ukv", [512, 4096])
                    self.W[p + "w_out"] = self.din(p + "w_out", [2048, D])
                else:
                    self.W[p + "w_in"] = self.din(p + "w_in", [D, 12352])
                    self.W[p + "conv"] = self.din(p + "conv", [128, 256])
                    self.W[p + "alog"] = self.din(p + "alog", [32, 1])
                    self.W[p + "dtb"] = self.din(p + "dtb", [32, 1])
                    self.W[p + "hn"] = self.din(p + "hn", [128, 1])
                    self.W[p + "w_out"] = self.din(p + "w_out", [4096, D])
            self.out = kb.dram(nc.dram_tensor("out", [S, D], F32, kind="ExternalOutput").ap())
            self.og = kb.dram(nc.dram_tensor("og_scr", [4096, S], BF16).ap())

            self.cst = kb.sb([128, NCONST], F32, "cst")
            self.cstb = kb.sb([128, 640], BF16, "cstb")
            self.small = kb.sb([128, 64], F32, "small")
            self.epsb = kb.sb([128, 1], F32, "epsb")
            self.pb = [kb.ps([128, 512], F32, "psum%d" % i) for i in range(8)]
            kb.dma("sp", self.cst[:, :], self.cst_d[:, :])
            kb.copy(self.cstb[:, :], self.cst[:, 0:640])
            kb.memset(self.epsb[:, :], EPS)
            self.onesF1 = kb.sb([128, 1], F32, "onesF1")
            kb.memset(self.onesF1[:, :], 1.0)
            self.identF = self.cst[:, 0:128]
            self.identB = self.cstb[:, 0:128]
            self.maskHG = self.cst[:, 128:256]
            self.maskSU = self.cst[:, 256:384]
            self.onesB = self.cstb[:, 384:512]
            self.scanmask = self.cst[:, 705:705 + 2048]

            first = True
            for l in self.layers:
                src = self.x if first else self.out
                first = False
                with contextlib.ExitStack() as lab:
                    kb.es = lab
                    self.uT = kb.sb([128, 16, S], BF16)
                    with contextlib.ExitStack() as les:
                        kb.es = les
                        self.phase_a(l, src)
                        kb.barrier()
                    with contextlib.ExitStack() as les:
                        kb.es = les
                        kind = LAYER_KIND[l]
                        if kind == 0:
                            self.hgrn(l)
                            nchunk = 16
                        elif kind == 1:
                            self.mla(l)
                            nchunk = 16
                        else:
                            self.gdn(l)
                            nchunk = 32
                        kb.barrier()
                with contextlib.ExitStack() as les:
                    kb.es = les
                    self.phase_c(l, src, nchunk)
                    kb.barrier()
                kb.es = es
            kb.wait_all("sp")
            kb.wait_all("pool")

    def rstd_from_ss(self, rs, ss, n):
        kb = self.kb
        P = rs.ap.shape[0]
        kb.act(rs, ss, AF.Sqrt, bias=vv(self.epsb[:, :], self.epsb.t[0:P, :]), scale=1.0 / n)
        kb.recip(rs, rs)

    def phase_a(self, l, src):
        kb = self.kb
        p = "l%d_" % l
        pre = kb.sb([128, 16], F32)
        kb.dma("sp", pre[:, :], self.W[p + "pre"][:, :])
        hb = [kb.sb([128, D], F32) for _ in range(2)]
        junk = kb.sb([128, D], BF16)
        ss = kb.sb([128, 2], F32)
        for i in range(16):
            ht = hb[i % 2]
            kb.dma("sp", ht[:, :], src.k(i, (slice(i * 128, (i + 1) * 128), slice(None))))
            kb.act(junk[:, :], ht[:, :], AF.Square, accum=ss[:, 0:1])
            self.rstd_from_ss(ss[:, 1:2], ss[:, 0:1], D)
            kb.ts(ht[:, :], ht[:, :], ss[:, 1:2], None, ALU.mult)
            for c4 in range(4):
                pbk = self.pb[c4 % 2]
                for j in range(4):
                    c = c4 * 4 + j
                    kb.tr(pbk[:, j * 128:(j + 1) * 128], ht[:, c * 128:(c + 1) * 128], self.identF)
                o = self.uT.k(("a", i), (slice(None), slice(c4 * 4, c4 * 4 + 4), slice(i * 128, (i + 1) * 128)))
                a = vv(pbk[:, :], pbk.t[:, :].rearrange("p (a b) -> p a b", b=128))
                b = vv(pre[:, :], pre.t[:, c4 * 4:c4 * 4 + 4].unsqueeze(2).to_broadcast([128, 4, 128]))
                kb.tt(o, a, b, ALU.mult)

    def phase_c(self, l, src, nchunk):
        kb = self.kb
        p = "l%d_" % l
        wo = self.W[p + "w_out"]
        C = nchunk
        postb = kb.sb([128, D], F32)
        kb.dma("sp", postb[:, :], vv(self.W[p + "post"][:, :], self.W[p + "post"].t[0:1, :].partition_broadcast(128).squeeze(1)))
        nog = 2 if C == 16 else 1
        ogs = [kb.sb([128, C, 512], BF16) for _ in range(nog)]
        if C == 16:
            wres = kb.sb([128, 16, D], BF16)
            for cb in range(4):
                kb.dma("pool", wres.k(cb, (slice(None), slice(None), slice(cb * 512, (cb + 1) * 512))),
                       vv(wo[:, :], wo.t[:, cb * 512:(cb + 1) * 512].rearrange("(c p) n -> p c n", p=128)))
        else:
            wbs = [kb.sb([128, 16, 512], BF16) for _ in range(4)]
        yb = kb.sb([128, 4, D], F32)
        hb = [kb.sb([128, D], F32) for _ in range(2)]
        junk = kb.sb([128, D], BF16)
        ss = kb.sb([128, 2], F32)
        wi = 0
        nh = C // 16
        for g in range(4):
            ogt = ogs[g % nog]
            kb.dma("sp", ogt[:, :, :], vv(self.og[:, :], self.og.t[0:C * 128, g * 512:(g + 1) * 512].rearrange("(c p) t -> p c t", p=128)))
            for cb in range(4):
                slabs = []
                if C != 16:
                    for hh in range(nh):
                        wb = wbs[wi % 4]
                        wi += 1
                        kb.dma("pool", wb[:, :, :], vv(wo[:, :], wo.t[hh * 2048:(hh + 1) * 2048, cb * 512:(cb + 1) * 512].rearrange("(c p) n -> p c n", p=128)))
                        slabs.append(wb)
                for t in range(4):
                    pbk = self.pb[(cb * 4 + t) % 4]
                    for c in range(C):
                        if C == 16:
                            wv_ = wres.k(cb, (slice(None), c, slice(cb * 512, (cb + 1) * 512)))
                        else:
                            wv_ = slabs[c // 16][:, c % 16, :]
                        kb.mm(pbk[:, :], ogt[:, c, t * 128:(t + 1) * 128], wv_, start=(c == 0), stop=(c == C - 1))
                    kb.copy(yb.k(t, (slice(None), t, slice(cb * 512, (cb + 1) * 512))), pbk[:, :], e=("act" if t % 2 else "dve"))
            for t in range(4):
                i = g * 4 + t
                ht = hb[i % 2]
                rows = (slice(i * 128, (i + 1) * 128), slice(None))
                kb.dma("sp", ht[:, :], src.k(i, rows))
                y = yb.k(t, (slice(None), t, slice(None)))
                kb.act(junk[:, :], y, AF.Square, accum=ss[:, 0:1])
                self.rstd_from_ss(ss[:, 1:2], ss[:, 0:1], D)
                kb.stt(y, y, ss[:, 1:2], postb[:, :], ALU.mult, ALU.mult)
                kb.tt(ht[:, :], ht[:, :], y, ALU.add)
                kb.dma("sp", self.out.k(i, rows), ht[:, :])

    def proj_fm(self, pbk, w, cols, g):
        kb = self.kb
        n = cols.stop - cols.start
        for kc in range(16):
            kb.mm(vv(pbk[:, :], pbk.t[0:n, :]), w[:, kc, cols], self.uT[:, kc, g * 512:(g + 1) * 512],
                  start=(kc == 0), stop=(kc == 15))

    def proj_fm_gen(self, pbk, w, cols, g, step=8):
        kb = self.kb
        n = cols.stop - cols.start
        for kc in range(16):
            kb.mm(vv(pbk[:, :], pbk.t[0:n, :]), w[:, kc, cols], self.uT[:, kc, g * 512:(g + 1) * 512],
                  start=(kc == 0), stop=(kc == 15))
            if kc % step == step - 1:
                yield

    def headnorm_gate_store(self, psO, hn, zs_g, row0, g, tmp, ogb, psN=None):
        kb = self.kb
        sq = tmp["sq"]
        kb.act(sq[:, :], psO, AF.Square)
        if psN is None:
            psN = self.pb[7]
        kb.mm(psN[:, :], self.onesB, sq[:, :])
        rb = tmp["rb"]
        kb.act(rb[:, :], psN[:, :], AF.Ln, bias=self.epsb[:, :], scale=1.0 / 128)
        kb.act(rb[:, :], rb[:, :], AF.Exp, scale=-0.5)
        kb.stt(rb[:, :], psO, hn, rb[:, :], ALU.mult, ALU.mult)
        kb.tt(ogb[:, :], rb[:, :], zs_g, ALU.mult)
        kb.dma("sp", self.og.k(("og", row0, g), (slice(row0, row0 + 128), slice(g * 512, (g + 1) * 512))), ogb[:, :])

    def hgrn(self, l):
        kb = self.kb
        p = "l%d_" % l
        win = self.W[p + "w_in"]
        lbp = kb.sb([128, 64], F32)
        kb.dma("sp", lbp[:, :], self.lb_d[:, :])
        kb.act(lbp[:, :], lbp[:, :], AF.Exp)
        den = kb.sb([128, 16], F32)
        lb = kb.sb([128, 16], F32)
        oml = kb.sb([128, 16], F32)
        kb.tt(den[:, :], lbp[:, 0:16], lbp[:, 16:32], ALU.add)
        kb.tt(den[:, :], den[:, :], lbp[:, 32:48], ALU.add)
        kb.tt(den[:, :], den[:, :], lbp[:, 48:64], ALU.add)
        kb.recip(den[:, :], den[:, :])
        kb.memset(lb[:, :], 0.0)
        for j in range(1, l + 1):
            kb.tt(lb[:, :], lb[:, :], lbp[:, j * 16:(j + 1) * 16], ALU.add)
        kb.tt(lb[:, :], lb[:, :], den[:, :], ALU.mult)
        kb.ts(oml[:, :], lb[:, :], -1.0, 1.0, ALU.mult, ALU.add)
        siga = kb.sb([128, 16], F32)
        sigc = kb.sb([128, 16], F32)
        kb.ts(siga[:, :], oml[:, :], 0.5, None, ALU.mult)
        kb.tt(sigc[:, :], lb[:, :], siga[:, :], ALU.add)
        hn = kb.sb([128, 1], F32)
        kb.dma("sp", hn[:, :], self.W[p + "hn"][:, :])

        wq = [kb.sb([128, 16, 128], BF16) for _ in range(2)]
        wf = [kb.sb([128, 16, 128], BF16) for _ in range(2)]
        wv = [kb.sb([128, 16, 128], BF16) for _ in range(2)]
        wz = [kb.sb([128, 16, 128], BF16) for _ in range(2)]
        t1 = kb.sb([128, S], F32)
        t2 = kb.sb([128, S], F32)
        t3 = kb.sb([128, S], F32)
        t4 = kb.sb([128, S], F32)
        kdt = kb.sb([128, S], BF16)
        qtS = [kb.sb([128, S], BF16) for _ in range(2)]
        ktS = [kb.sb([128, S], BF16) for _ in range(2)]
        zsS = [kb.sb([128, S], BF16) for _ in range(2)]
        vtS = [kb.sb([128, 16, 128], BF16) for _ in range(2)]
        ktokS = [kb.sb([128, 16, 128], BF16) for _ in range(2)]
        elS = [kb.sb([128, 32], F32) for _ in range(2)]
        AT = [kb.sb([128, 128], BF16) for _ in range(2)]
        St = kb.sb([128, 128], F32)
        SbS = [kb.sb([128, 128], BF16) for _ in range(4)]
        tmp = {"sq": kb.sb([128, 512], BF16), "rb": kb.sb([128, 512], F32)}
        ogb = [kb.sb([128, 512], BF16) for _ in range(2)]
        scale = 128 ** -0.5

        def wload(h):
            s_ = h % 2
            for j, wb in enumerate((wq, wf, wv, wz)):
                c0 = j * 2048 + h * 128
                kb.dma("pool", wb[s_][:, :, :], vv(win[:, :], win.t[:, c0:c0 + 128].rearrange("(c p) n -> p c n", p=128)))

        def prep(h):
            s_ = h % 2
            qt, kt, zs, vt, ktok, elb = qtS[s_], ktS[s_], zsS[s_], vtS[s_], ktokS[s_], elS[s_]
            ah = siga[:, h:h + 1]
            ch = sigc[:, h:h + 1]
            for g in range(4):
                gs = slice(g * 512, (g + 1) * 512)
                G_ = lambda b_: b_.k(g, (slice(None), gs))
                pA = self.pb[0]
                yield from self.proj_fm_gen(pA, wf[s_], slice(0, 128), g)
                kb.act(G_(t1), pA[:, :], AF.Tanh, scale=0.5)
                pQ = self.pb[1]
                yield from self.proj_fm_gen(pQ, wq[s_], slice(0, 128), g)
                kb.act(G_(t4), pQ[:, :], AF.Silu)
                kb.ts(G_(t1), G_(t1), ah, ch, ALU.mult, ALU.add)
                pZ = self.pb[2]
                yield from self.proj_fm_gen(pZ, wz[s_], slice(0, 128), g)
                kb.act(G_(zs), pZ[:, :], AF.Silu)
                kb.ts(G_(t2), G_(t1), -1.0, 1.0, ALU.mult, ALU.add)
                yield
            for g in range(4):
                gs = slice(g * 512, (g + 1) * 512)
                G_ = lambda b_: b_.k(g, (slice(None), gs))
                kb.act(G_(t1), G_(t1), AF.Ln)
                yield
                kb.scan(G_(t3), vv(self.cst[:, :], self.cst.t[:, 705 + g * 512:705 + (g + 1) * 512]), G_(t1))
                yield
                kb.act(G_(t1), G_(t3), AF.Exp)
                yield
                kb.stt(G_(qt), G_(t4), scale, G_(t1), ALU.mult, ALU.mult)
                yield
                kb.act(G_(t4), G_(t3), AF.Exp, scale=-1.0)
                yield
                kb.tt(G_(kt), G_(t2), G_(t4), ALU.mult)
                yield
                kb.tt(vv(G_(kdt), kdt.t[:, gs].rearrange("p (a b) -> p a b", b=64)),
                      vv(G_(kt), kt.t[:, gs].rearrange("p (a b) -> p a b", b=64)),
                      vv(G_(t1), t1.t[:, g * 512 + 63:(g + 1) * 512:64].unsqueeze(2).to_broadcast([128, 8, 64])),
                      ALU.mult)
                kb.copy(elb.k(g, (slice(None), slice(g * 8, (g + 1) * 8))), vv(G_(t1), t1.t[:, g * 512 + 63:(g + 1) * 512:64]))
                yield
            for g in range(4):
                pV = self.pb[g % 2]
                for j in range(4):
                    i = g * 4 + j
                    for kc in range(16):
                        kb.mm(pV[:, j * 128:(j + 1) * 128], self.uT[:, kc, i * 128:(i + 1) * 128], wv[s_][:, kc, :],
                              start=(kc == 0), stop=(kc == 15))
                    yield
                kb.copy(vt.k(g, (slice(None), slice(g * 4, g * 4 + 4), slice(None))),
                        vv(pV[:, :], pV.t[:, :].rearrange("p (a b) -> p a b", b=128)), e="act")
                pT = self.pb[2]
                pTb = pT.t[:, :].bitcast(BF16)
                for j in range(4):
                    i = g * 4 + j
                    kb.tr(vv(pT[:, :], pTb[:, j * 128:(j + 1) * 128]), kdt.k(g, (slice(None), slice(i * 128, (i + 1) * 128))), self.identB)
                yield
                kb.copy(ktok.k(g, (slice(None), slice(g * 4, g * 4 + 4), slice(None))),
                        vv(pT[:, :], pTb[:, 0:512].rearrange("p (a b) -> p a b", b=128)))
                yield

        def loop(h):
            s_ = h % 2
            qt, kt, zs, vt, ktok, elb = qtS[s_], ktS[s_], zsS[s_], vtS[s_], ktokS[s_], elS[s_]
            kb.memset(St[:, :], 0.0)
            for n_ in range(4):
                kb.memset(SbS[n_][:, :], 0.0)

            def stage1(i):
                g = i // 4
                TB = self.pb[6 + i % 2]
                ts_ = slice(i * 128, (i + 1) * 128)
                kb.mm(TB[:, 0:128], kt.k(g, (slice(None), ts_)), qt.k(g, (slice(None), ts_)))
                kb.mm(TB[:, 128:256], ktok.k(g, (slice(0, 64), i, slice(None))), vt.k(g, (slice(0, 64), i, slice(None))))
                YB = self.pb[3]
                kb.mm(YB[:, (i % 2) * 128:(i % 2) * 128 + 128], ktok.k(g, (slice(64, 128), i, slice(None))), vt.k(g, (slice(64, 128), i, slice(None))))

            def stage2(i):
                g = i // 4
                TB = self.pb[6 + i % 2]
                kb.tt(AT[i % 2][:, :], TB[:, 0:128], self.maskHG, ALU.mult)
                for c in range(2):
                    n_ = 2 * i + c
                    el = elb.k(g, (slice(None), slice(n_, n_ + 1)))
                    src_ = TB[:, 128:256] if c == 0 else self.pb[3][:, (i % 2) * 128:(i % 2) * 128 + 128]
                    kb.stt(St[:, :], St[:, :], el, src_, ALU.mult, ALU.add)
                    kb.copy(SbS[n_ % 4][:, :], St[:, :], e="pool")

            def stage3(i):
                g = i // 4
                j = i % 4
                pO = self.pb[4 + g % 2]
                kb.mm(pO[:, j * 128:(j + 1) * 128], vt.k(g, (slice(None), i, slice(None))), AT[i % 2][:, :], start=True, stop=False)
                for c in range(2):
                    n_ = 2 * i + c
                    cs = slice(i * 128 + c * 64, i * 128 + (c + 1) * 64)
                    kb.mm(pO[:, j * 128 + c * 64:j * 128 + (c + 1) * 64], SbS[(n_ - 1) % 4][:, :], qt.k(g, (slice(None), cs)),
                          start=False, stop=(c == 1))

            stage1(0)
            yield
            stage2(0)
            yield
            for i in range(16):
                if i + 1 < 16:
                    stage1(i + 1)
                    yield
                stage3(i)
                yield
                if i + 1 < 16:
                    stage2(i + 1)
                    yield
                if i % 4 == 3:
                    g = i // 4
                    self.headnorm_gate_store(self.pb[4 + g % 2][:, :], hn[:, 0:1], zs.k(g, (slice(None), slice(g * 512, (g + 1) * 512))),
                                             h * 128, g, tmp, ogb[g % 2], psN=self.pb[3])
                    yield

        def run_gens(gens):
            alive = [True] * len(gens)
            while any(alive):
                for e_, gn in enumerate(gens):
                    if alive[e_]:
                        try:
                            next(gn)
                        except StopIteration:
                            alive[e_] = False

        wload(0)
        wload(1)
        run_gens([prep(0)])
        for h in range(16):
            if h + 2 < 16:
                wload(h + 2)
            gens = [loop(h)]
            if h + 1 < 16:
                gens.append(prep(h + 1))
            run_gens(gens)

    def rope(self, pin, out, gs, cos2, sin2, tmp):
        kb = self.kb
        tf, ta, tb2 = tmp
        kb.copy(tf[:, :], pin, e="act")
        pR = self.pb[4]
        pr = vv(pR[:, :], pR.t[0:64, :])
        kb.mm(pr, vv(self.cst[:, :], self.cst.t[0:64, 640:704]), tf[:, :])
        kb.tt(ta[:, :], tf[:, :], vv(cos2[:, :], cos2.t[:, gs]), ALU.mult)
        kb.tt(tb2[:, :], pr, vv(sin2[:, :], sin2.t[:, gs]), ALU.mult)
        kb.tt(out, ta[:, :], tb2[:, :], ALU.add)

    def sin_reduced(self, out, ang, t1, ti):
        kb = self.kb
        PI = float(np.pi)
        kb.ts(t1[:, :], ang, 1.0 / (2 * PI), None, ALU.mult)
        kb.copy(ti[:, :], t1[:, :])
        kb.copy(t1[:, :], ti[:, :])
        kb.stt(out, t1[:, :], -2 * PI, ang, ALU.mult, ALU.add)
        kb.ts(t1[:, :], out, PI, None, ALU.is_gt)
        kb.stt(out, t1[:, :], -2 * PI, out, ALU.mult, ALU.add)
        kb.ts(t1[:, :], out, -PI, None, ALU.is_lt)
        kb.stt(out, t1[:, :], 2 * PI, out, ALU.mult, ALU.add)
        kb.ts(out, out, -3.1415925, 3.1415925, ALU.max, ALU.min)
        kb.act(out, out, AF.Sin)

    def mla(self, l):
        kb = self.kb
        p = "l%d_" % l
        win = self.W[p + "w_in"]
        wuq = self.W[p + "w_uq"]
        wukv = self.W[p + "w_ukv"]
        cqn = kb.sb([128, 4, S], BF16)
        ckvn = kb.sb([128, 4, S], BF16)
        krT = kb.sb([64, S], BF16)
        cos2 = kb.sb([64, S], F32)
        sin2 = kb.sb([64, S], F32)
        nw = kb.sb([128, 8], F32)
        kb.dma("sp", nw[:, 0:4], self.W[p + "qn"][:, :])
        kb.dma("sp", nw[:, 4:8], self.W[p + "kvn"][:, :])
        top = kb.es
        with contextlib.ExitStack() as s0:
            kb.es = s0
            posi = kb.sb([64, S], I32)
            ang = kb.sb([64, S], F32)
            t1 = kb.sb([64, S], F32)
            kb.dma("sp", posi[:, :], vv(self.pos[:, :], self.pos.t[0:1, :].partition_broadcast(64).squeeze(1)))
            kb.copy(ang[:, :], posi[:, :])
            kb.ts(ang[:, :], ang[:, :], vv(self.cst[:, :], self.cst.t[0:64, 704:705]), None, ALU.mult)
            self.sin_reduced(sin2[:, :], ang[:, :], t1, posi)
            kb.ts(ang[:, :], ang[:, :], float(np.pi / 2), None, ALU.add)
            self.sin_reduced(cos2[:, :], ang[:, :], t1, posi)
            kb.barrier()
        with contextlib.ExitStack() as s1:
            kb.es = s1
            wc = [kb.sb([128, 16, 512], BF16) for _ in range(2)]
            wkr = kb.sb([128, 16, 64], BF16)
            cf = kb.sb([128, 4, 512], F32)
            sq = kb.sb([128, 512], BF16)
            rb = kb.sb([128, 512], F32)
            rt = [kb.sb([64, 512], F32) for _ in range(3)]
            for j in range(2):
                kb.dma("pool", wc[j][:, :, :], vv(win[:, :], win.t[:, j * 512:(j + 1) * 512].rearrange("(c p) n -> p c n", p=128)))
            kb.dma("pool", wkr[:, :, :], vv(win[:, :], win.t[:, 1024:1088].rearrange("(c p) n -> p c n", p=128)))
            for g in range(4):
                gs = slice(g * 512, (g + 1) * 512)
                for j, dst in enumerate((cqn, ckvn)):
                    psN = self.pb[7]
                    for c in range(4):
                        pA = self.pb[c % 2]
                        self.proj_fm(pA, wc[j], slice(c * 128, (c + 1) * 128), g)
                        kb.copy(cf[:, c, :], pA[:, :], e="act")
                        kb.act(sq[:, :], pA[:, :], AF.Square)
                        kb.mm(psN[:, :], self.onesB, sq[:, :], start=(c == 0), stop=(c == 3))
                    kb.act(rb[:, :], psN[:, :], AF.Ln, bias=self.epsb[:, :], scale=1.0 / 512)
                    kb.act(rb[:, :], rb[:, :], AF.Exp, scale=-0.5)
                    for c in range(4):
                        kb.stt(dst.k(g, (slice(None), c, gs)), cf[:, c, :], nw[:, j * 4 + c:j * 4 + c + 1], rb[:, :], ALU.mult, ALU.mult)
                pA = self.pb[2]
                self.proj_fm(pA, wkr, slice(0, 64), g)
                self.rope(vv(pA[:, :], pA.t[0:64, :]), krT.k(g, (slice(None), gs)), gs, cos2, sin2, rt)
            kb.barrier()
        with contextlib.ExitStack() as s2:
            kb.es = s2
            wq_ = [kb.sb([128, 4, 192], BF16) for _ in range(2)]
            wkv_ = [kb.sb([128, 4, 256], BF16) for _ in range(2)]
            wz = [kb.sb([128, 16, 128], BF16) for _ in range(2)]
            qn = kb.sb([128, S], BF16)
            qr = kb.sb([64, S], BF16)
            kn = kb.sb([128, S], BF16)
            vt = kb.sb([128, 16, 128], BF16)
            zs = kb.sb([128, S], BF16)
            PT = [kb.sb([128, 512], BF16) for _ in range(2)]
            rt = [kb.sb([64, 512], F32) for _ in range(3)]
            rden = kb.sb([128, 512], F32)
            o1 = kb.sb([128, 512], F32)
            ogb = [kb.sb([128, 512], BF16) for _ in range(2)]
            scale = 192 ** -0.5

            def wload(h):
                s = h % 2
                kb.dma("pool", wq_[s][:, :, :], vv(wuq[:, :], wuq.t[:, h * 192:(h + 1) * 192].rearrange("(c p) n -> p c n", p=128)))
                kb.dma("pool", wkv_[s][:, :, :], vv(wukv[:, :], wukv.t[:, h * 256:(h + 1) * 256].rearrange("(c p) n -> p c n", p=128)))
                c0 = 1088 + h * 128
                kb.dma("pool", wz[s][:, :, :], vv(win[:, :], win.t[:, c0:c0 + 128].rearrange("(c p) n -> p c n", p=128)))

            wload(0)
            for h in range(16):
                s = h % 2
                if h + 1 < 16:
                    wload(h + 1)
                for g in range(4):
                    gs = slice(g * 512, (g + 1) * 512)
                    pA = self.pb[0]
                    for kc in range(4):
                        kb.mm(pA[:, :], wq_[s][:, kc, 0:128], cqn.k(g, (slice(None), kc, gs)), start=(kc == 0), stop=(kc == 3))
                    kb.copy(qn.k(g, (slice(None), gs)), pA[:, :], e="act")
                    pB = self.pb[1]
                    pb64 = vv(pB[:, :], pB.t[0:64, :])
                    for kc in range(4):
                        kb.mm(pb64, wq_[s][:, kc, 128:192], cqn.k(g, (slice(None), kc, gs)), start=(kc == 0), stop=(kc == 3))
                    self.rope(pb64, qr.k(g, (slice(None), gs)), gs, cos2, sin2, rt)
                    pA = self.pb[0]
                    for kc in range(4):
                        kb.mm(pA[:, :], wkv_[s][:, kc, 0:128], ckvn.k(g, (slice(None), kc, gs)), start=(kc == 0), stop=(kc == 3))
                    kb.copy(kn.k(g, (slice(None), gs)), pA[:, :])
                    pV = self.pb[1]
                    for j in range(4):
                        i = g * 4 + j
                        for kc in range(4):
                            kb.mm(pV[:, j * 128:(j + 1) * 128], ckvn.k(g, (slice(None), kc, slice(i * 128, (i + 1) * 128))),
                                  wkv_[s][:, kc, 128:256], start=(kc == 0), stop=(kc == 3))
                    kb.copy(vt.k(g, (slice(None), slice(g * 4, g * 4 + 4), slice(None))),
                            vv(pV[:, :], pV.t[:, :].rearrange("p (a b) -> p a b", b=128)), e="act")
                    pZ = self.pb[0]
                    self.proj_fm(pZ, wz[s], slice(0, 128), g)
                    kb.act(zs.k(g, (slice(None), gs)), pZ[:, :], AF.Silu)
                seq = [(g, j) for g in range(4) for j in range(4 * g + 4)]

                def emit_qk(idx):
                    g, j = seq[idx]
                    c0 = max(0, j * 128 - g * 512)
                    ks = slice(j * 128, (j + 1) * 128)
                    qs = slice(g * 512 + c0, (g + 1) * 512)
                    gk = j // 4
                    pS = self.pb[2 + idx % 2]
                    kb.mm(pS[:, c0:512], kn.k(gk, (slice(None), ks)), qn.k(g, (slice(None), qs)), start=True, stop=False)
                    kb.mm(pS[:, c0:512], krT.k(gk, (slice(None), ks)), qr.k(g, (slice(None), qs)), start=False, stop=True)

                def emit_rest(idx):
                    g, j = seq[idx]
                    gs = slice(g * 512, (g + 1) * 512)
                    nj = 4 * g + 4
                    c0 = max(0, j * 128 - g * 512)
                    gk = j // 4
                    pS = self.pb[2 + idx % 2]
                    pt = PT[idx % 2]
                    pO = self.pb[4 + g % 2]
                    pD = self.pb[6 + g % 2]
                    kb.act(pt[:, c0:512], pS[:, c0:512], AF.Exp, scale=scale)
                    if j >= 4 * g:
                        kb.memset(vv(pt[:, :], pt.t[64:128, c0:c0 + 64]), 0.0)
                    kb.mm(pO[:, c0:512], vt.k(gk, (slice(None), j, slice(None))), pt[:, c0:512], start=(j == 0), stop=(j == nj - 1))
                    kb.mm(pD[:, c0:512], self.onesB, pt[:, c0:512], start=(j == 0), stop=(j == nj - 1))
                    if j == nj - 1:
                        kb.act(rden[:, :], pD[:, :], AF.Ln)
                        kb.act(rden[:, :], rden[:, :], AF.Exp, scale=-1.0)
                        kb.tt(o1[:, :], pO[:, :], rden[:, :], ALU.mult)
                        ob = ogb[g % 2]
                        kb.tt(ob[:, :], o1[:, :], zs.k(g, (slice(None), gs)), ALU.mult)
                        kb.dma("sp", self.og.k(("og", h * 128, g), (slice(h * 128, h * 128 + 128), gs)), ob[:, :])

                emit_qk(0)
                for idx in range(len(seq)):
                    if idx + 1 < len(seq):
                        emit_qk(idx + 1)
                    emit_rest(idx)
            kb.barrier()
        kb.es = top

    def gdn(self, l):
        kb = self.kb
        p = "l%d_" % l
        win = self.W[p + "w_in"]
        top = kb.es
        cw = kb.sb([128, 256], F32)
        kb.dma("sp", cw[:, :], self.W[p + "conv"][:, :])
        hn = kb.sb([128, 1], F32)
        kb.dma("sp", hn[:, :], self.W[p + "hn"][:, :])
        gcT = kb.sb([32, S], F32)
        egcT = kb.sb([32, S], F32)
        egc_tok = kb.sb([128, 16, 32], F32)
        edl_tok = kb.sb([128, 16, 32], F32)
        beta_tok = kb.sb([128, 16, 32], F32)
        nbeta_tok = kb.sb([128, 16, 32], F32)
        ngc_tok = kb.sb([128, 16, 32], F32)
        NEGHG = kb.sb([128, 128], BF16)
        kb.ts(NEGHG[:, :], self.maskHG, 1e6, -1e6, ALU.mult, ALU.add)
        Up = self.cst[:, 512:640]
        id32 = vv(self.cst[:, :], self.cst.t[0:32, 0:32])

        def wslab(dst, c0, n=128):
            kb.dma("pool", dst[:, :, :], vv(win[:, :], win.t[:, c0:c0 + n].rearrange("(c p) n -> p c n", p=128)))

        with contextlib.ExitStack() as s0:
            kb.es = s0
            wa = kb.sb([128, 16, 32], BF16)
            wb = kb.sb([128, 16, 32], BF16)
            gT = kb.sb([32, S], F32)
            bT = kb.sb([32, S], F32)
            prm = kb.sb([32, 4], F32)
            gtk = kb.sb([128, 64], F32)
            wslab(wa, 12288, 32)
            wslab(wb, 12320, 32)
            kb.dma("sp", prm[:, 0:1], self.W[p + "alog"][:, :])
            kb.dma("sp", prm[:, 1:2], self.W[p + "dtb"][:, :])
            kb.act(prm[:, 2:3], prm[:, 0:1], AF.Exp)
            kb.ts(prm[:, 2:3], prm[:, 2:3], -1.0, None, ALU.mult)
            for g in range(4):
                gs = slice(g * 512, (g + 1) * 512)
                pA = self.pb[0]
                self.proj_fm(pA, wa, slice(0, 32), g)
                pa32 = vv(pA[:, :], pA.t[0:32, :])
                kb.act(gT[:, gs], pa32, AF.Exp, bias=prm[:, 1:2])
                kb.act(gT[:, gs], gT[:, gs], AF.Ln, bias=vv(self.onesF1[:, :], self.onesF1.t[0:32, :]))
                kb.ts(gT[:, gs], gT[:, gs], prm[:, 2:3], None, ALU.mult)
                pB = self.pb[1]
                self.proj_fm(pB, wb, slice(0, 32), g)
                kb.act(bT[:, gs], vv(pB[:, :], pB.t[0:32, :]), AF.Sigmoid)
            kb.scan(gcT[:, :], vv(self.cst[:, :], self.cst.t[0:32, 705:705 + S]), gT[:, :])
            kb.act(egcT[:, :], gcT[:, :], AF.Exp)
            for i in range(16):
                tl = slice(i * 128, (i + 1) * 128)
                pT = self.pb[2]
                kb.tr(pT[:, 0:32], gT[:, tl], id32)
                kb.tr(pT[:, 32:64], bT[:, tl], id32)
                kb.copy(gtk[:, :], pT[:, 0:64], e="act")
                kb.copy(beta_tok[:, i, :], gtk[:, 32:64])
                kb.ts(nbeta_tok[:, i, :], gtk[:, 32:64], -1.0, None, ALU.mult)
                pG = self.pb[3]
                kb.mm(pG[:, 0:32], self.maskHG, gtk[:, 0:32])
                kb.mm(pG[:, 32:64], Up, gtk[:, 0:32])
                kb.ts(ngc_tok[:, i, :], pG[:, 0:32], -1.0, None, ALU.mult)
                kb.act(egc_tok[:, i, :], pG[:, 0:32], AF.Exp)
                kb.act(edl_tok[:, i, :], pG[:, 32:64], AF.Exp)
                kb.tt(edl_tok[:, i, :], edl_tok[:, i, :], gtk[:, 32:64], ALU.mult)
            kb.barrier()
        kb.es = top
        wq = kb.sb([128, 16, 128], BF16)
        wk = kb.sb([128, 16, 128], BF16)
        wvz = [kb.sb([128, 16, 128], BF16) for _ in range(2)]
        xp = kb.sb([128, 3 + S], F32)
        xs = kb.sb([128, S], F32)
        knb = kb.sb([128, S], BF16)
        qnb = kb.sb([128, S], BF16)
        kb.memset(xp[:, 0:3], 0.0)

        def r32(v):
            return vv(v, v.ap.bitcast(F32R))

        HB = []
        for e in range(2):
            b = {}
            b["vT"] = kb.sb([128, S], BF16)
            b["zs"] = kb.sb([128, S], BF16)
            b["qtil"] = kb.sb([128, S], BF16)
            b["egl"] = kb.sb([128, 32], F32)
            b["selh"] = kb.sb([32, 128], F32)
            b["tmp"] = {"sq": kb.sb([128, 512], BF16), "rb": kb.sb([128, 512], F32)}
            b["ogb"] = [kb.sb([128, 512], BF16) for _ in range(1)]
            b["dm"] = kb.sb([128, 128], F32)
            b["DT"] = kb.sb([128, 128], F32)
            b["t1"] = b["dm"]
            b["attnT"] = kb.sb([128, 128], BF16)
            b["PTb"] = [kb.sb([128, 128], F32) for _ in range(6)]
            b["W"] = [kb.sb([128, 384], F32) for _ in range(2)]
            b["kd"] = kb.sb([128, 128], BF16)
            b["wT"] = kb.sb([128, 128], BF16)
            b["vnew"] = kb.sb([128, 128], BF16)
            b["St"] = kb.sb([128, 128], F32)
            b["Sb"] = kb.sb([128, 128], BF16)
            b["bank"] = self.pb[4 * e:4 * e + 4]
            HB.append(b)
        tmpq = HB[0]["tmp"]

        def conv_silu(w, blk, dst):
            for g in range(4):
                pA = self.pb[g % 2]
                self.proj_fm(pA, w, slice(0, 128), g)
                kb.copy(xp.k(g, (slice(None), slice(3 + g * 512, 3 + (g + 1) * 512))), pA[:, :], e="act")
            kb.ts(xs[:, :], xp[:, 3:3 + S], cw[:, blk * 4 + 3:blk * 4 + 4], None, ALU.mult)
            for j in (2, 1, 0):
                kb.stt(xs[:, :], xp[:, j:j + S], cw[:, blk * 4 + j:blk * 4 + j + 1], xs[:, :], ALU.mult, ALU.add)
            kb.act(dst, xs[:, :], AF.Silu)

        def l2n(dst, sc):
            for g in range(4):
                gs = slice(g * 512, (g + 1) * 512)
                sq = HB[g % 2]["tmp"]["sq"]
                rb = HB[g % 2]["tmp"]["rb"]
                kb.act(sq[:, :], xs[:, gs], AF.Square)
                psN = self.pb[7]
                kb.mm(psN[:, :], self.onesB, sq[:, :])
                kb.act(rb[:, :], psN[:, :], AF.Ln, bias=self.epsb[:, :], scale=1.0)
                kb.act(rb[:, :], rb[:, :], AF.Exp, scale=-0.5)
                kb.stt(dst[:, gs], xs[:, gs], sc, rb[:, :], ALU.mult, ALU.mult)

        def head_prep(hv, e, b):
            wslab(wvz[0], 4096 + hv * 128)
            wslab(wvz[1], 8192 + hv * 128)
            conv_silu(wvz[0], 32 + hv, b["vT"][:, :])
            kb.copy(b["selh"][:, :], vv(self.cst[:, :], self.cst.t[0:32, hv:hv + 1].to_broadcast([32, 128])))
            for g in range(4):
                gs = slice(g * 512, (g + 1) * 512)
                pZ = self.pb[g % 2]
                self.proj_fm(pZ, wvz[1], slice(0, 128), g)
                kb.act(b["zs"][:, gs], pZ[:, :], AF.Silu)
                pE = self.pb[2 + g % 2]
                kb.mm(pE[:, :], b["selh"][:, :], egcT[:, gs])
                kb.tt(b["qtil"][:, gs], qnb[:, gs], pE[:, :], ALU.mult)
                kb.copy(b["egl"][:, g * 8:(g + 1) * 8], vv(pE[:, :], pE.t[:, 63:512:64]), e="act")
            kb.memset(b["St"][:, :], 0.0)
            kb.memset(b["Sb"][:, :], 0.0)

        def head_tiles(hv, e, b):
            B0, B1, B2, B3 = b["bank"]
            dm, DT, t1, attnT, PTb = b["dm"], b["DT"], b["t1"], b["attnT"], b["PTb"]
            kd, wT, vnew, St, Sb = b["kd"], b["wT"], b["vnew"], b["St"], b["Sb"]
            hcol = slice(hv, hv + 1)
            for g in range(4):
                gs = slice(g * 512, (g + 1) * 512)
                pO = B3
                for j in range(4):
                    i = g * 4 + j
                    tl = slice(i * 128, (i + 1) * 128)
                    r_gc = B0[:, 0:128]
                    r_G = B0[:, 128:256]
                    r_KQ = B0[:, 256:384]
                    kb.mm(r_gc, b["selh"][:, :], gcT[:, tl], start=True, stop=False)
                    kb.mm(r_gc, self.identB, NEGHG[:, :], start=False, stop=True)
                    kb.mm(r_G, knb[:, tl], knb[:, tl])
                    kb.mm(r_KQ, knb[:, tl], qnb[:, tl])
                    yield
                    kb.act(DT[:, :], r_gc, AF.Exp, bias=ngc_tok[:, i, hcol])
                    yield
                    kb.stt(t1[:, :], r_G, nbeta_tok[:, i, hcol], DT[:, :], ALU.mult, ALU.mult)
                    kb.tt(r32(PTb[0][:, :]), t1[:, :], self.maskSU, ALU.mult)
                    kb.stt(attnT[:, :], r_KQ, beta_tok[:, i, hcol], DT[:, :], ALU.mult, ALU.mult)
                    yield
                    b2b = B2.t[:, :].bitcast(BF16)
                    r_N = B2[:, 0:128]
                    r_vt = V(b2b[:, 256:384], B2, None)
                    r_kt = V(b2b[:, 384:512], B2, None)
                    r_app = B2[:, 0:256]
                    r_wT = B2[:, 256:384]
                    r_st = B2[:, 384:512]
                    kb.tr(r_N, PTb[0][:, :], self.identF)
                    kb.tr(r_kt, knb[:, tl], self.identB)
                    kb.tr(r_vt, b["vT"][:, tl], self.identB)
                    yield
                    W = b["W"]
                    kb.copy(r32(W[0][:, 256:384]), r_N, e="act")
                    kb.copy(r32(W[0][:, 0:128]), r_vt, e="act")
                    kb.ts(r32(W[0][:, 128:256]), r_kt, egc_tok[:, i, hcol], None, ALU.mult)
                    kb.ts(kd[:, :], r_kt, edl_tok[:, i, hcol], None, ALU.mult)
                    yield
                    PT = PTb[0]
                    cur = 0
                    r_pt = B2[:, 0:128]
                    for lvl in range(1, 6):
                        if lvl <= 4:
                            kb.mm(B1[:, 0:384], r32(PT[:, :]), r32(W[cur][:, 0:384]))
                        else:
                            kb.mm(B1[:, 0:256], r32(PT[:, :]), r32(W[cur][:, 0:256]))
                        kb.mm(r_pt, r32(W[cur][:, 256:384]), r32(PT[:, :]))
                        yield
                        kb.tt(r32(W[1 - cur][:, 0:256]), W[cur][:, 0:256], B1[:, 0:256], ALU.add)
                        kb.copy(r32(PTb[lvl][:, :]), r_pt, e="act")
                        if lvl <= 4:
                            kb.copy(r32(W[1 - cur][:, 256:384]), B1[:, 256:384], e="act")
                        cur = 1 - cur
                        PT = PTb[lvl]
                        yield
                    kb.mm(B1[:, 0:256], r32(PTb[5][:, :]), r32(W[cur][:, 0:256]))
                    yield
                    kb.tt(r32(W[1 - cur][:, 0:256]), W[cur][:, 0:256], B1[:, 0:256], ALU.add)
                    cur = 1 - cur
                    uw = W[cur]
                    yield
                    kb.tr(r_wT, uw[:, 128:256], self.identF)
                    yield
                    kb.copy(wT[:, :], r_wT, e="act")
                    yield
                    for c in range(2):
                        ps_ = slice(c * 64, (c + 1) * 64)
                        r_w = vv(B0[:, :], B0.t[ps_, 384:512])
                        kb.mm(r_w, wT[:, ps_], Sb[:, :])
                        yield
                        kb.tt(vv(vnew[:, :], vnew.t[ps_, :]), vv(uw[:, :], uw.t[ps_, 0:128]), r_w, ALU.subtract)
                        yield
                        oc = pO[:, j * 128 + c * 64:j * 128 + (c + 1) * 64]
                        kb.mm(oc, vv(vnew[:, :], vnew.t[ps_, :]), vv(attnT[:, :], attnT.t[ps_, ps_]), start=True, stop=False)
                        kb.mm(oc, Sb[:, :], b["qtil"][:, i * 128 + c * 64:i * 128 + (c + 1) * 64], start=False, stop=True)
                        kb.mm(r_st, vv(kd[:, :], kd.t[ps_, :]), vv(vnew[:, :], vnew.t[ps_, :]))
                        yield
                        egl = b["egl"][:, i * 2 + c:i * 2 + c + 1]
                        kb.stt(St[:, :], St[:, :], egl, r_st, ALU.mult, ALU.add)
                        yield
                        kb.copy(Sb[:, :], St[:, :], e="act")
                        yield
                sq = b["tmp"]["sq"]
                rb = b["tmp"]["rb"]
                ob = b["ogb"][0]
                kb.act(sq[:, :], pO[:, :], AF.Square)
                yield
                kb.mm(B1[:, :], self.onesB, sq[:, :])
                yield
                kb.act(rb[:, :], B1[:, :], AF.Ln, bias=self.epsb[:, :], scale=1.0 / 128)
                yield
                kb.act(rb[:, :], rb[:, :], AF.Exp, scale=-0.5)
                yield
                kb.stt(rb[:, :], pO[:, :], hn[:, 0:1], rb[:, :], ALU.mult, ALU.mult)
                yield
                kb.tt(ob[:, :], rb[:, :], b["zs"][:, gs], ALU.mult)
                kb.dma("sp", self.og.k(("og", hv * 128, g), (slice(hv * 128, hv * 128 + 128), gs)), ob[:, :])
                yield

        for hq in range(16):
            wslab(wq, hq * 128)
            wslab(wk, 2048 + hq * 128)
            conv_silu(wq, hq, xs[:, :])
            l2n(qnb, 128 ** -0.5)
            conv_silu(wk, 16 + hq, xs[:, :])
            l2n(knb, 1.0)
            gens = []
            for e in range(2):
                head_prep(2 * hq + e, e, HB[e])
            for e in range(2):
                gens.append(head_tiles(2 * hq + e, e, HB[e]))
            alive = [True, True]
            while any(alive):
                for e in range(2):
                    if alive[e]:
                        try:
                            next(gens[e])
                        except StopIteration:
                            alive[e] = False


def prep_inputs(inputs, b, layers):
    f = np.ascontiguousarray
    consts, sel = make_consts()
    m = {"x": f(inputs["x"][b]), "pos": f(inputs["positions"][b:b + 1].astype(np.int32)), "consts": consts, "sel": sel}
    lb = np.asarray(inputs["hgrn_lb"], np.float32)
    m["lbp"] = f(lb.reshape(4, 16, 128).transpose(2, 0, 1).reshape(128, 64))
    for l in range(4):
        p = "l%d_" % l
        m[p + "pre"] = f(np.asarray(inputs[p + "pre_norm"], np.float32).reshape(16, 128).T)
        m[p + "post"] = f(np.asarray(inputs[p + "post_norm"], np.float32).reshape(1, D))
        kind = LAYER_KIND[l]
        m[p + "w_in"] = f(inputs[p + "w_in"])
        m[p + "w_out"] = f(inputs[p + "w_out"])
        if kind == 0:
            m[p + "hn"] = f(np.asarray(inputs[p + "head_norm"], np.float32).reshape(128, 1))
        elif kind == 1:
            m[p + "qn"] = f(np.asarray(inputs[p + "q_norm"], np.float32).reshape(4, 128).T)
            m[p + "kvn"] = f(np.asarray(inputs[p + "kv_norm"], np.float32).reshape(4, 128).T)
            m[p + "w_uq"] = f(inputs[p + "w_uq"])
            m[p + "w_ukv"] = f(inputs[p + "w_ukv"])
        else:
            cw = np.asarray(inputs[p + "conv_w"], np.float32)
            m[p + "conv"] = f(cw.reshape(4, 64, 128).transpose(2, 1, 0).reshape(128, 256))
            m[p + "alog"] = f(np.asarray(inputs[p + "a_log"], np.float32).reshape(32, 1))
            m[p + "dtb"] = f(np.asarray(inputs[p + "dt_bias"], np.float32).reshape(32, 1))
            m[p + "hn"] = f(np.asarray(inputs[p + "head_norm"], np.float32).reshape(128, 1))
    return m


_PROG = {}


def run(inputs, layers=(0, 1, 2, 3), cores=8):
    key = tuple(layers)
    if key not in _PROG:
        _PROG[key] = Prog(list(layers))
    prog = _PROG[key]
    in_maps = [prep_inputs(inputs, b, layers) for b in range(cores)]
    res = run_bass_kernel_spmd(prog.nc, in_maps, core_ids=list(range(cores)))
    return np.stack([np.asarray(r["out"], np.float32) for r in res.results], axis=0)


def kernel(x, positions, hgrn_lb,
           l0_pre_norm, l0_post_norm, l0_w_in, l0_head_norm, l0_w_out,
           l1_pre_norm, l1_post_norm, l1_w_in, l1_q_norm, l1_kv_norm, l1_w_uq, l1_w_ukv, l1_w_out,
           l2_pre_norm, l2_post_norm, l2_w_in, l2_conv_w, l2_a_log, l2_dt_bias, l2_head_norm, l2_w_out,
           l3_pre_norm, l3_post_norm, l3_w_in, l3_head_norm, l3_w_out):
    inputs = dict(
        x=x, positions=positions, hgrn_lb=hgrn_lb,
        l0_pre_norm=l0_pre_norm, l0_post_norm=l0_post_norm, l0_w_in=l0_w_in, l0_head_norm=l0_head_norm, l0_w_out=l0_w_out,
        l1_pre_norm=l1_pre_norm, l1_post_norm=l1_post_norm, l1_w_in=l1_w_in, l1_q_norm=l1_q_norm, l1_kv_norm=l1_kv_norm,
        l1_w_uq=l1_w_uq, l1_w_ukv=l1_w_ukv, l1_w_out=l1_w_out,
        l2_pre_norm=l2_pre_norm, l2_post_norm=l2_post_norm, l2_w_in=l2_w_in, l2_conv_w=l2_conv_w, l2_a_log=l2_a_log,
        l2_dt_bias=l2_dt_bias, l2_head_norm=l2_head_norm, l2_w_out=l2_w_out,
        l3_pre_norm=l3_pre_norm, l3_post_norm=l3_post_norm, l3_w_in=l3_w_in, l3_head_norm=l3_head_norm, l3_w_out=l3_w_out)
    inputs = {k: np.asarray(v) for k, v in inputs.items()}
    return run(inputs)
```
